# Optimizing a Trainium2 kernel written in Bass

```python
import numpy as np
import jax, jax.numpy as jnp
from jax import lax

D_MODEL = 1024
BATCH = 2
SEQ = 16384
DEPTH = 1
DEC_BATCH = 8
DEC_SEQ = 8192
PAST_LEN = 128

GRID_W = 64
D_CONV = D_MODEL // 2
CONV_K = 31
N_HEADS = 8
HEAD_DIM = D_MODEL // (2 * N_HEADS)
D_ATT = N_HEADS * HEAD_DIM
KH_MAX = 8
KW = 16
QW = 16
UW = QW + KW
N_CB = GRID_W // QW
D_FF = 4 * D_MODEL
N_IN = 2 * D_CONV + 3 * D_ATT + 2 * D_MODEL
ALPHA = (2.0 * DEPTH) ** 0.25
BETA = (8.0 * DEPTH) ** -0.25
LN_EPS = 1e-5
NEG_INF = -1e30

kernel_name = "hybrid_conformer_natten_encoder"


def layer_norm(x, g, b):
    xf = x.astype(jnp.float32)
    mu = jnp.mean(xf, axis=-1, keepdims=True)
    xc = xf - mu
    var = jnp.mean(xc * xc, axis=-1, keepdims=True)
    y = xc * lax.rsqrt(var + LN_EPS) * g.astype(jnp.float32) + b.astype(jnp.float32)
    return y.astype(x.dtype)


def _column_tables():
    qcol = np.arange(N_CB)[:, None] * QW + np.arange(QW)[None, :]
    cs = np.clip(qcol - KW // 2, 0, GRID_W - KW)
    u0 = np.clip(np.arange(N_CB) * QW - KW // 2, 0, GRID_W - UW)
    ucol = u0[:, None] + np.arange(UW)[None, :]
    valid = (ucol[:, None, :] >= cs[:, :, None]) & (ucol[:, None, :] < cs[:, :, None] + KW)
    dc_idx = np.clip(ucol[:, None, :] - qcol[:, :, None] + KW - 1, 0, 2 * KW - 2)
    return ucol, valid, dc_idx


def neighbourhood_attention(q, k, v, rpb):
    b, t, h, dh = q.shape
    rows = t // GRID_W
    kh = min(KH_MAX, rows)
    ucol, valid, dc_idx = _column_tables()
    q = (q * jnp.asarray(dh ** -0.5, q.dtype)).reshape(b, rows, GRID_W, h, dh)
    k = k.reshape(b, rows, GRID_W, h, dh)
    v = v.reshape(b, rows, GRID_W, h, dh)
    mask = valid[:, :, None, :]
    dc = dc_idx[:, :, None, :]

    def row_fn(r):
        rs = jnp.clip(r - kh // 2, 0, rows - kh)
        q_blk = lax.dynamic_index_in_dim(q, r, axis=1, keepdims=False)
        q_blk = q_blk.reshape(b, N_CB, QW, h, dh)
        k_blk = lax.dynamic_slice_in_dim(k, rs, kh, axis=1)[:, :, ucol]
        v_blk = lax.dynamic_slice_in_dim(v, rs, kh, axis=1)[:, :, ucol]
        s = jnp.einsum('bjqhd,bkjchd->bhjqkc', q_blk, k_blk).astype(jnp.float32)
        dr_idx = rs - r + jnp.arange(kh) + (KH_MAX - 1)
        bias = rpb[:, dr_idx[None, None, :, None], dc].astype(jnp.float32)
        s = jnp.where(mask, s + bias, NEG_INF)
        p = jax.nn.softmax(s.reshape(b, h, N_CB, QW, kh * UW), axis=-1)
        p = p.reshape(s.shape).astype(v.dtype)
        o = jnp.einsum('bhjqkc,bkjchd->bjqhd', p, v_blk)
        return o.reshape(b, GRID_W, h, dh)

    out = lax.map(row_fn, jnp.arange(rows))
    return out.transpose(1, 0, 2, 3, 4).reshape(b, t, h * dh)


def depthwise_conv(u, w, bias):
    y = lax.conv_general_dilated(
        u, w[:, None, :], window_strides=(1,),
        padding=[(CONV_K // 2, CONV_K // 2)],
        dimension_numbers=('NWC', 'WIO', 'NWC'),
        feature_group_count=u.shape[-1])
    return y + bias


def encoder_layer(x, w_in, dw_kernel, dw_bias, conv_ln_g, conv_ln_b, w_conv_out,
                  rpb, w_att_out, w_out, ln1_g, ln1_b, w_up, w_down, ln2_g, ln2_b):
    b, t, _ = x.shape
    proj = x @ w_in
    splits = np.cumsum([2 * D_CONV, D_ATT, D_ATT, D_ATT, D_MODEL])
    u, q, k, v, g_conv, g_att = jnp.split(proj, splits, axis=-1)
    u_a, u_g = jnp.split(u, 2, axis=-1)
    c = u_a * jax.nn.sigmoid(u_g)
    c = depthwise_conv(c, dw_kernel, dw_bias)
    c = jax.nn.swish(layer_norm(c, conv_ln_g, conv_ln_b))
    conv_out = c @ w_conv_out
    a = neighbourhood_attention(q.reshape(b, t, N_HEADS, HEAD_DIM),
                                k.reshape(b, t, N_HEADS, HEAD_DIM),
                                v.reshape(b, t, N_HEADS, HEAD_DIM), rpb)
    att_out = a @ w_att_out
    mixed = jax.nn.sigmoid(g_conv) * conv_out + jax.nn.sigmoid(g_att) * att_out
    x = layer_norm(ALPHA * x + mixed @ w_out, ln1_g, ln1_b)
    hdn = jnp.square(jax.nn.relu(x @ w_up))
    x = layer_norm(ALPHA * x + hdn @ w_down, ln2_g, ln2_b)
    return x


def setup_inputs(seed: int = 0) -> dict:
    key = jax.random.key(seed)
    ks = jax.random.split(key, 20)
    f32 = jnp.float32
    nrm = lambda kk, shape: jax.random.normal(kk, shape, f32)
    x_prompt = nrm(ks[0], (BATCH, SEQ, D_MODEL))
    x_sample = nrm(ks[1], (DEC_BATCH, DEC_SEQ, D_MODEL))
    col_scale = jnp.concatenate([
        jnp.ones((2 * D_CONV + 2 * D_ATT,), f32),
        jnp.full((D_ATT,), BETA, f32),
        jnp.ones((2 * D_MODEL,), f32)])
    w_in = nrm(ks[2], (DEPTH, D_MODEL, N_IN)) * (D_MODEL ** -0.5) * col_scale
    dw_kernel = nrm(ks[3], (DEPTH, CONV_K, D_CONV)) * (CONV_K ** -0.5)
    dw_bias = nrm(ks[4], (DEPTH, D_CONV)) * 0.02
    conv_ln_g = 1.0 + 0.05 * nrm(ks[5], (DEPTH, D_CONV))
    conv_ln_b = 0.02 * nrm(ks[6], (DEPTH, D_CONV))
    w_conv_out = nrm(ks[7], (DEPTH, D_CONV, D_MODEL)) * (D_CONV ** -0.5)
    rpb = 0.02 * nrm(ks[8], (DEPTH, N_HEADS, 2 * KH_MAX - 1, 2 * KW - 1))
    w_att_out = nrm(ks[9], (DEPTH, D_ATT, D_MODEL)) * (D_ATT ** -0.5)
    w_out = nrm(ks[10], (DEPTH, D_MODEL, D_MODEL)) * (D_MODEL ** -0.5) * BETA
    ln1_g = 1.0 + 0.05 * nrm(ks[11], (DEPTH, D_MODEL))
    ln1_b = 0.02 * nrm(ks[12], (DEPTH, D_MODEL))
    w_up = nrm(ks[13], (DEPTH, D_MODEL, D_FF)) * (D_MODEL ** -0.5) * BETA
    w_down = nrm(ks[14], (DEPTH, D_FF, D_MODEL)) * (D_FF ** -0.5) * BETA
    ln2_g = 1.0 + 0.05 * nrm(ks[15], (DEPTH, D_MODEL))
    ln2_b = 0.02 * nrm(ks[16], (DEPTH, D_MODEL))
    return {"x_prompt": x_prompt, "x_sample": x_sample, "w_in": w_in,
            "dw_kernel": dw_kernel, "dw_bias": dw_bias, "conv_ln_g": conv_ln_g,
            "conv_ln_b": conv_ln_b, "w_conv_out": w_conv_out, "rpb": rpb,
            "w_att_out": w_att_out, "w_out": w_out, "ln1_g": ln1_g, "ln1_b": ln1_b,
            "w_up": w_up, "w_down": w_down, "ln2_g": ln2_g, "ln2_b": ln2_b}


def reference(x_prompt, x_sample, w_in, dw_kernel, dw_bias, conv_ln_g, conv_ln_b,
              w_conv_out, rpb, w_att_out, w_out, ln1_g, ln1_b, w_up, w_down,
              ln2_g, ln2_b):
    y_prompt = x_prompt
    y_sample = x_sample
    for l in range(DEPTH):
        params = (w_in[l], dw_kernel[l], dw_bias[l], conv_ln_g[l], conv_ln_b[l],
                  w_conv_out[l], rpb[l], w_att_out[l], w_out[l], ln1_g[l], ln1_b[l],
                  w_up[l], w_down[l], ln2_g[l], ln2_b[l])
        y_prompt = encoder_layer(y_prompt, *params)
        y_sample = encoder_layer(y_sample, *params)
    return (y_prompt, y_sample)
```

```python
import contextlib
import numpy as np
import ml_dtypes
import concourse.bass as bass
import concourse.mybir as mybir
from concourse.bass_utils import run_bass_kernel_spmd

F32 = mybir.dt.float32
BF16 = mybir.dt.bfloat16
AF = mybir.ActivationFunctionType
ALU = mybir.AluOpType

D = 1024
NIN = 4608
DFF = 4096
CK = 31
NH = 8
TT = 512
HALO = 256
ALPHA = float(2.0 ** 0.25)
EPS = 1e-5
NEG = -30000.0
GRID_W = 64

SAME_ENGINE_SYNC = True


class Reg:
    def __init__(self, name=""):
        self.name = name
        self.w = []
        self.r = []
        self.prev = []

    def newgen(self):
        self.prev = self.w + self.r
        self.w = []
        self.r = []


class Sch:
    ENG = ["pe", "act", "dve", "pool", "sp"]

    def __init__(self, nc, stack, tag, dma_sems):
        self.nc = nc
        self.q = {e: [] for e in self.ENG}
        self.sem = {}
        self.cnt = {}
        self.waited = {e: {} for e in self.ENG}
        for k in ["pe", "act", "dve", "pool"] + list(dma_sems) + ["bar"]:
            self.sem[k] = stack.enter_context(nc.semaphore(f"{tag}_{k}"))
            self.cnt[k] = 0

    def _waits(self, eng, deps):
        need = {}
        for (k, v) in deps:
            if k == "pe" and eng == "pe":
                continue
            if (not SAME_ENGINE_SYNC) and k == eng:
                continue
            if v > need.get(k, 0):
                need[k] = v
        out = []
        for k, v in need.items():
            if self.waited[eng].get(k, 0) < v:
                self.waited[eng][k] = v
                out.append((k, v))
        return out

    @staticmethod
    def _deps(R, W, Wacc):
        deps = []
        for r in R:
            deps += r.w
        for r in W:
            deps += r.w + r.r + r.prev
        for r in Wacc:
            deps += r.prev
        return deps

    @staticmethod
    def _commit(t, R, W, Wacc):
        for r in R:
            r.r.append(t)
        for r in W:
            r.w = [t]
            r.r = []
            r.prev = []
        for r in Wacc:
            r.w.append(t)

    def op(self, eng, fn, R=(), W=(), Wacc=()):
        return self.multi(eng, [fn], R, W, Wacc)

    def multi(self, eng, fns, R=(), W=(), Wacc=()):
        waits = self._waits(eng, self._deps(R, W, Wacc))
        n = len(fns)
        self.cnt[eng] += 1
        t = (eng, self.cnt[eng])
        for i, fn in enumerate(fns):
            self.q[eng].append((waits if i == 0 else [], fn, (eng, 1) if i == n - 1 else None))
        self._commit(t, R, W, Wacc)
        return t

    def dma(self, eng, fn, sem, R=(), W=(), Wacc=()):
        waits = self._waits(eng, self._deps(R, W, Wacc))
        self.cnt[sem] += 16
        t = (sem, self.cnt[sem])
        self.q[eng].append((waits, fn, (sem, 16)))
        self._commit(t, R, W, Wacc)
        return t

    def finish(self):
        deps = [(k, v) for k, v in self.cnt.items() if v > 0 and k != "bar"]
        waits = self._waits("sp", deps)
        self.q["sp"].append((waits, lambda e: e.nop(), ("bar", 1)))
        for e in ["pe", "act", "dve", "pool"]:
            self.q[e].append(([("bar", 1)], None, None))

    def replay(self):
        with self.nc.Block() as blk:
            def run(key):
                def body(e):
                    for (waits, fn, sig) in self.q[key]:
                        for (k, v) in waits:
                            e.wait_ge(self.sem[k], v)
                        if fn is None:
                            continue
                        ins = fn(e)
                        if sig is not None:
                            ins.then_inc(self.sem[sig[0]], sig[1])
                return body
            blk.tensor(run("pe"))
            blk.scalar(run("act"))
            blk.vector(run("dve"))
            blk.gpsimd(run("pool"))
            blk.sync(run("sp"))


def MM(out, lhsT, rhs, start, stop):
    return lambda e: e.matmul(out, lhsT=lhsT, rhs=rhs, start=start, stop=stop)


def TR(out, in_, ident):
    return lambda e: e.transpose(out=out, in_=in_, identity=ident)


def ACT(out, in_, func, bias=None, scale=None):
    def f(e):
        kw = {}
        if bias is not None:
            kw["bias"] = bias
        if scale is not None:
            kw["scale"] = scale
        return e.activation(out=out, in_=in_, func=func, **kw)
    return f


def COPY(out, in_):
    return lambda e: e.tensor_copy(out=out, in_=in_)


def TTOP(out, a, b, op):
    return lambda e: e.tensor_tensor(out=out, in0=a, in1=b, op=op)


def TSOP(out, a, s1, s2, op0, op1):
    return lambda e: e.tensor_scalar(out=out, in0=a, scalar1=s1, scalar2=s2, op0=op0, op1=op1)


def STT(out, a, s, b, op0, op1):
    return lambda e: e.scalar_tensor_tensor(out=out, in0=a, scalar=s, in1=b, op0=op0, op1=op1)


def DMA(out, in_):
    return lambda e: e.dma_start(out=out, in_=in_)


def load_cast(S, dst3, src_dram2, stage, Rstage, Rdst, ncols, rows=128, chunk=4096):
    for c0 in range(0, ncols, chunk):
        cw = min(chunk, ncols - c0)
        S.dma("pool", DMA(dst3[0:rows, c0:c0 + cw], src_dram2[0:rows, c0:c0 + cw]), "wl", Wacc=[Rdst])


def build_program(SA, SB):
    EA = SA + 2 * HALO
    EB = SB + 2 * HALO
    TEXT = EA + EB
    TREAL = SA + SB
    segs = [(0, SA), (EA, SB)]
    real_tiles = []
    o0 = 0
    for (so, s_) in segs:
        nt = s_ // TT
        for i in range(nt):
            real_tiles.append((so + HALO + i * TT, o0 + i * TT, i == 0, i == nt - 1))
        o0 += s_

    nc = bass.Bass("TRN2", target_bir_lowering=False)

    def din(name, shape, dt=F32):
        return nc.dram_tensor(name, list(shape), dt, kind="ExternalInput").ap()

    def dscr(name, shape, dt):
        return nc.dram_tensor(name, list(shape), dt, kind="Internal").ap()

    x_ext = din("x_ext", [TEXT, D])
    w_in_r = din("w_in_r", [128, 8 * NIN])
    w_co_r = din("w_co_r", [128, 4 * D])
    w_ao_r = din("w_ao_r", [128, 4 * D])
    w_out_r = din("w_out_r", [128, 8 * D])
    w_up_r = din("w_up_r", [128, 8 * DFF])
    w_dn_r = din("w_dn_r", [128, 32 * D])
    dwk_r = din("dwk_r", [128, 4 * CK])
    cpar_r = din("cpar_r", [128, 12])
    lnp_r = din("lnp_r", [128, 4 * D])
    ident_r = din("ident_r", [128, 128])
    rpbx_r = din("rpbx_r", [128, NH * 9 * 128])
    bcm_r = din("bcm_r", [128, 9 * 128])
    km_r = din("km_r", [32, TEXT], BF16)
    qm_r = din("qm_r", [32, TEXT], BF16)
    out_d = nc.dram_tensor("out", [TREAL, D], F32, kind="ExternalOutput").ap()

    cT_d = dscr("cT_d", [512, TEXT], BF16)
    qT_d = dscr("qT_d", [512, TEXT], BF16)
    kT_d = dscr("kT_d", [512, TEXT], BF16)
    v_d = dscr("v_d", [TEXT, 512], BF16)
    gc_d = dscr("gc_d", [D, TEXT], BF16)
    ga_d = dscr("ga_d", [D, TEXT], BF16)
    G1_d = dscr("G1_d", [D, TEXT], BF16)
    x1_d = dscr("x1_d", [TEXT, D], F32)

    def fm(ap, t0, n):
        return ap.rearrange("(c p) t -> p c t", p=128)[:, :, t0:t0 + n]

    with contextlib.ExitStack() as st:
        sb = lambda name, shape, dt: st.enter_context(nc.sbuf_tensor(name, list(shape), dt))
        ps = lambda name, shape, dt=F32: st.enter_context(nc.psum_tensor(name, list(shape), dt))
        w_in = sb("w_in", [128, 8 * NIN], BF16)
        ident = sb("ident1", [128, 128], BF16)
        xs = [sb(f"xs{i}", [128, D], F32) for i in range(4)]
        xb = [sb(f"xb{i}", [128, D], BF16) for i in range(4)]
        xT = [sb(f"xT{i}", [128, 8 * TT], BF16) for i in range(2)]
        c_sb = [sb(f"c_sb{i}", [128, 4 * TT], BF16) for i in range(2)]
        q_sb = [sb(f"q_sb{i}", [128, 4 * TT], BF16) for i in range(2)]
        k_sb = [sb(f"k_sb{i}", [128, 4 * TT], BF16) for i in range(2)]
        v_sb = [sb(f"v_sb{i}", [128, 4 * 512], BF16) for i in range(2)]
        g_sb = [sb(f"g_sb{i}", [128, 16 * TT], BF16) for i in range(2)]
        sg = [sb(f"sg{i}", [128, TT], F32) for i in range(2)]
        ptr = [ps(f"ptr{i}", [128, 1024], BF16) for i in range(2)]
        pmm = [ps(f"pmm{i}", [128, 512]) for i in range(4)]

        with contextlib.ExitStack() as st2:
            S = Sch(nc, st2, "w1", ["wl", "st0", "st1", "id"])
            stage = [xs[0], xs[2]]
            Rstage = [Reg(), Reg()]
            Rw = Reg()
            load_cast(S, w_in, w_in_r, stage, Rstage, Rw, 8 * NIN, chunk=1024)
            load_cast(S, ident, ident_r, None, None, Reg(), 128)
            S.finish()
            S.replay()

        with contextlib.ExitStack() as st2:
            dsems = [f"xs{i}" for i in range(4)] + [f"{n}{i}" for n in ["c", "q", "k", "v", "gc", "ga"] for i in range(2)]
            S = Sch(nc, st2, "p1", dsems)
            Rxs = [Reg() for _ in range(4)]
            RxT = [Reg() for _ in range(2)]
            Rptr = [Reg() for _ in range(2)]
            Rpmm = [Reg() for _ in range(4)]
            Rsg = [Reg() for _ in range(2)]
            Rout = {n: [Reg(), Reg()] for n in ["c", "q", "k", "v", "g"]}
            w3 = w_in[:].rearrange("p (k n) -> p k n", k=8)
            state = {"pm": 0, "pt": 0, "sg": 0}

            def nxt_pm():
                i = state["pm"] % 4
                state["pm"] += 1
                return i

            ntile = TEXT // TT
            Rxb = [Reg() for _ in range(4)]
            xT3s = [xT[i][:].rearrange("p (k t) -> p k t", k=8) for i in range(2)]

            def p1_load(j):
                t0 = j * TT
                for b in range(4):
                    s_ = b
                    S.dma("sp", DMA(xs[s_][:], x_ext[t0 + 128 * b: t0 + 128 * b + 128, :]), f"xs{s_}", W=[Rxs[s_]])
                    S.op("dve", COPY(xb[s_][:], xs[s_][:]), R=[Rxs[s_]], W=[Rxb[s_]])

            def p1_T(j):
                sl = j % 2
                RxT[sl].newgen()
                for b in range(4):
                    pi = state["pt"] % 2
                    state["pt"] += 1
                    S.multi("pe", [TR(ptr[pi][:, i * 128:(i + 1) * 128], xb[b][:, i * 128:(i + 1) * 128], ident[:])
                                   for i in range(8)], R=[Rxb[b]], W=[Rptr[pi]])
                    src = ptr[pi][:].rearrange("p (k t) -> p k t", k=8)
                    dst = xT3s[sl][:, :, b * 128:(b + 1) * 128]
                    if b % 2 == 0:
                        S.op("act", ACT(dst, src, AF.Copy), R=[Rptr[pi]], Wacc=[RxT[sl]])
                    else:
                        S.op("dve", COPY(dst, src), R=[Rptr[pi]], Wacc=[RxT[sl]])

            def p1_mm(j, part):
                t0 = j * TT
                sl = j % 2
                xT3 = xT3s[sl]

                def proj(n0, ncols):
                    i = nxt_pm()
                    S.multi("pe", [MM(pmm[i][:, 0:TT], w3[:, kc, n0:n0 + ncols], xT3[:, kc, :], kc == 0, kc == 7)
                                   for kc in range(8)], R=[RxT[sl]], W=[Rpmm[i]])
                    return i

                c3 = c_sb[sl][:].rearrange("p (c t) -> p c t", c=4)
                q3 = q_sb[sl][:].rearrange("p (c t) -> p c t", c=4)
                k3 = k_sb[sl][:].rearrange("p (c t) -> p c t", c=4)
                v3 = v_sb[sl][:].rearrange("p (c t) -> p c t", c=4)
                g3 = g_sb[sl][:].rearrange("p (c t) -> p c t", c=16)
                if part == 0:
                    for n in ["c", "q", "k", "v", "g"]:
                        Rout[n][sl].newgen()
                    for i in range(4):
                        pg = proj(512 + 128 * i, 128)
                        si = state["sg"] % 2
                        state["sg"] += 1
                        S.op("act", ACT(sg[si][:], pmm[pg][:, 0:TT], AF.Sigmoid), R=[Rpmm[pg]], W=[Rsg[si]])
                        pa = proj(128 * i, 128)
                        S.op("dve", TTOP(c3[:, i, :], pmm[pa][:, 0:TT], sg[si][:], ALU.mult),
                             R=[Rpmm[pa], Rsg[si]], Wacc=[Rout["c"][sl]])
                    for i in range(4):
                        pq = proj(1024 + 128 * i, 128)
                        S.op("act", ACT(q3[:, i, :], pmm[pq][:, 0:TT], AF.Identity, scale=0.125), R=[Rpmm[pq]], Wacc=[Rout["q"][sl]])
                        pk = proj(1536 + 128 * i, 128)
                        S.op("dve", COPY(k3[:, i, :], pmm[pk][:, 0:TT]), R=[Rpmm[pk]], Wacc=[Rout["k"][sl]])
                    S.dma("pool", DMA(fm(cT_d, t0, TT), c3), f"c{sl}", R=[Rout["c"][sl]])
                    S.dma("pool", DMA(fm(qT_d, t0, TT), q3), f"q{sl}", R=[Rout["q"][sl]])
                    S.dma("pool", DMA(fm(kT_d, t0, TT), k3), f"k{sl}", R=[Rout["k"][sl]])
                else:
                    for b in range(4):
                        i = nxt_pm()
                        S.multi("pe", [MM(pmm[i][:, 0:512], xT3[:, kc, b * 128:(b + 1) * 128], w3[:, kc, 2048:2560], kc == 0, kc == 7)
                                       for kc in range(8)], R=[RxT[sl]], W=[Rpmm[i]])
                        if b % 2 == 0:
                            S.op("dve", COPY(v3[:, b, :], pmm[i][:, 0:512]), R=[Rpmm[i]], Wacc=[Rout["v"][sl]])
                        else:
                            S.op("act", ACT(v3[:, b, :], pmm[i][:, 0:512], AF.Copy), R=[Rpmm[i]], Wacc=[Rout["v"][sl]])
                    for i in range(16):
                        pg = proj(2560 + 128 * i, 128)
                        S.op("act", ACT(g3[:, i, :], pmm[pg][:, 0:TT], AF.Sigmoid), R=[Rpmm[pg]], Wacc=[Rout["g"][sl]])
                    S.dma("pool", DMA(v_d[t0:t0 + TT, :].rearrange("(b p) f -> p b f", p=128), v3), f"v{sl}", R=[Rout["v"][sl]])
                    S.dma("pool", DMA(fm(gc_d, t0, TT), g3[:, 0:8, :]), f"gc{sl}", R=[Rout["g"][sl]])
                    S.dma("pool", DMA(fm(ga_d, t0, TT), g3[:, 8:16, :]), f"ga{sl}", R=[Rout["g"][sl]])

            p1_load(0)
            p1_T(0)
            for j in range(ntile):
                if j + 1 < ntile:
                    p1_load(j + 1)
                p1_mm(j, 0)
                if j + 1 < ntile:
                    p1_T(j + 1)
                p1_mm(j, 1)
            S.finish()
            S.replay()

    with contextlib.ExitStack() as st:
        sb = lambda name, shape, dt: st.enter_context(nc.sbuf_tensor(name, list(shape), dt))
        ps = lambda name, shape, dt=F32: st.enter_context(nc.psum_tensor(name, list(shape), dt))
        dwd = sb("dwd", [128, 4 * CK * 128], BF16)
        w_co = sb("w_co", [128, 4 * D], BF16)
        identb = sb("identb2", [128, 128], F32)
        onesf = sb("onesf", [128, 128], F32)
        dwk = sb("dwk", [128, 4 * CK], F32)
        cpar = sb("cpar", [128, 12], F32)
        stage = [sb(f"stg2{i}", [128, 2048], F32) for i in range(2)]
        c_in = [sb(f"c_in{i}", [128, 4 * (TT + 30)], BF16) for i in range(2)]
        gc_in = [sb(f"gc_in{i}", [128, 8 * TT], BF16) for i in range(2)]
        y_sb = sb("y_sb", [128, 4 * TT], F32)
        ysq = sb("ysq", [128, 4 * TT], F32)
        m_sb = sb("m_sb", [128, TT], F32)
        r_sb = sb("r_sb", [128, TT], F32)
        t_sb = [sb(f"t_sb{i}", [128, TT], F32) for i in range(2)]
        z_sb = sb("z_sb", [128, 4 * TT], BF16)
        G1_sb = [sb(f"G1_sb{i}", [128, 8 * TT], BF16) for i in range(2)]
        py = [ps(f"py{i}", [128, 512]) for i in range(4)]
        pst = [ps(f"pst{i}", [128, 512]) for i in range(2)]
        pco = [ps(f"pco{i}", [128, 512]) for i in range(2)]

        with contextlib.ExitStack() as st2:
            S = Sch(nc, st2, "w2", ["wl", "st0", "st1", "m0", "m1", "m2"])
            Rstage = [Reg(), Reg()]
            Rw = Reg()
            load_cast(S, w_co, w_co_r, stage, Rstage, Rw, 4 * D)
            Rid, Rdwk, Rcp = Reg(), Reg(), Reg()
            S.dma("sp", DMA(identb[:], ident_r), "m0", W=[Rid])
            S.dma("sp", DMA(dwk[:], dwk_r), "m1", W=[Rdwk])
            S.dma("sp", DMA(cpar[:], cpar_r), "m2", W=[Rcp])
            Rones = Reg()
            S.op("dve", lambda e: e.memset(onesf[:], 1.0 / 512.0), W=[Rones])
            Rdwd = Reg()
            engs = ["dve", "pool"]
            for ch in range(4):
                for j in range(CK):
                    o = (ch * CK + j) * 128
                    S.op("dve",
                         TSOP(dwd[:, o:o + 128], identb[:], dwk[:, ch * CK + j:ch * CK + j + 1], None, ALU.mult, ALU.bypass),
                         R=[Rid, Rdwk], Wacc=[Rdwd])
            S.finish()
            S.replay()

        with contextlib.ExitStack() as st2:
            dsems = [f"ci{i}" for i in range(2)] + [f"gi{i}" for i in range(2)] + [f"go{i}" for i in range(2)]
            S = Sch(nc, st2, "p2", dsems)
            Rci = [Reg(), Reg()]
            Rgi = [Reg(), Reg()]
            Rgo = [Reg(), Reg()]
            Rpy = [Reg() for _ in range(4)]
            Rpst = [Reg(), Reg()]
            Rpco = [Reg(), Reg()]
            Ry, Rysq, Rm, Rr, Rz = Reg(), Reg(), Reg(), Reg(), Reg()
            Rt = [Reg(), Reg()]
            dwd4 = dwd[:].rearrange("p (c j m) -> p c j m", c=4, j=CK)
            wco3 = w_co[:].rearrange("p (k n) -> p k n", k=4)
            y3 = y_sb[:].rearrange("p (c t) -> p c t", c=4)
            ysq3 = ysq[:].rearrange("p (c t) -> p c t", c=4)
            z3 = z_sb[:].rearrange("p (c t) -> p c t", c=4)
            NT2 = len(real_tiles)

            def v2(ti):
                sl = ti % 2
                return (sl, c_in[sl][:].rearrange("p (c t) -> p c t", c=4),
                        gc_in[sl][:].rearrange("p (c t) -> p c t", c=8),
                        G1_sb[sl][:].rearrange("p (c t) -> p c t", c=8))

            def p2_load(ti):
                e0 = real_tiles[ti][0]
                sl, ci3, gi3, go3 = v2(ti)
                S.dma("sp", DMA(ci3, fm(cT_d, e0 - 15, TT + 30)), f"ci{sl}", W=[Rci[sl]])
                S.dma("sp", DMA(gi3, fm(gc_d, e0, TT)), f"gi{sl}", W=[Rgi[sl]])

            def p2_conv(ti):
                sl, ci3, gi3, go3 = v2(ti)
                for ch in range(4):
                    S.multi("pe", [MM(py[ch][:, 0:TT], dwd4[:, ch, j, :], ci3[:, ch, j:j + TT], j == 0, j == CK - 1)
                                   for j in range(CK)], R=[Rci[sl]], W=[Rpy[ch]])

            def p2_evac(ti):
                Ry.newgen()
                Rysq.newgen()
                for ch in range(4):
                    S.op("act", ACT(y3[:, ch, :], py[ch][:, 0:TT], AF.Identity, bias=cpar[:, ch:ch + 1]), R=[Rpy[ch]], Wacc=[Ry])
                    S.op("act", ACT(ysq3[:, ch, :], py[ch][:, 0:TT], AF.Square, bias=cpar[:, ch:ch + 1]), R=[Rpy[ch]], Wacc=[Rysq])

            def p2_stats(ti):
                S.multi("pe", [MM(pst[0][:, 0:TT], onesf[:], y3[:, ch, :], ch == 0, ch == 3) for ch in range(4)], R=[Ry], W=[Rpst[0]])
                S.multi("pe", [MM(pst[1][:, 0:TT], onesf[:], ysq3[:, ch, :], ch == 0, ch == 3) for ch in range(4)], R=[Rysq], W=[Rpst[1]])

            def p2_chain(ti):
                S.op("dve", COPY(m_sb[:], pst[0][:, 0:TT]), R=[Rpst[0]], W=[Rm])
                S.op("dve", TTOP(r_sb[:], m_sb[:], m_sb[:], ALU.mult), R=[Rm], W=[Rr])
                S.op("dve", STT(r_sb[:], pst[1][:, 0:TT], EPS, r_sb[:], ALU.add, ALU.subtract), R=[Rpst[1]], W=[Rr])
                S.op("act", ACT(r_sb[:], r_sb[:], AF.Sqrt), W=[Rr])
                S.op("dve", lambda e: e.reciprocal(out=r_sb[:], in_=r_sb[:]), W=[Rr])
                Rz.newgen()
                for ch in range(4):
                    ts = t_sb[ch % 2]
                    S.op("dve", TTOP(ts[:], y3[:, ch, :], m_sb[:], ALU.subtract), R=[Ry, Rm], W=[Rt[ch % 2]])
                    S.op("dve", TTOP(ts[:], ts[:], r_sb[:], ALU.mult), R=[Rr], W=[Rt[ch % 2]])
                    S.op("act", ACT(z3[:, ch, :], ts[:], AF.Silu, bias=cpar[:, 8 + ch:9 + ch], scale=cpar[:, 4 + ch:5 + ch]),
                         R=[Rt[ch % 2]], Wacc=[Rz])

            def p2_out(ti):
                e0 = real_tiles[ti][0]
                sl, ci3, gi3, go3 = v2(ti)
                Rgo[sl].newgen()
                for n in range(8):
                    pi = n % 2
                    S.multi("pe", [MM(pco[pi][:, 0:TT], wco3[:, kc, n * 128:(n + 1) * 128], z3[:, kc, :], kc == 0, kc == 3)
                                   for kc in range(4)], R=[Rz], W=[Rpco[pi]])
                    S.op("dve", TTOP(go3[:, n, :], pco[pi][:, 0:TT], gi3[:, n, :], ALU.mult), R=[Rpco[pi], Rgi[sl]], Wacc=[Rgo[sl]])
                S.dma("pool", DMA(fm(G1_d, e0, TT), go3), f"go{sl}", R=[Rgo[sl]])

            p2_load(0)
            if NT2 > 1:
                p2_load(1)
            p2_conv(0)
            p2_evac(0)
            for ti in range(NT2):
                p2_stats(ti)
                if ti + 1 < NT2:
                    p2_conv(ti + 1)
                p2_chain(ti)
                p2_out(ti)
                if ti + 1 < NT2:
                    p2_evac(ti + 1)
                if ti + 2 < NT2:
                    p2_load(ti + 2)
            S.finish()
            S.replay()

    with contextlib.ExitStack() as st:
        sb = lambda name, shape, dt: st.enter_context(nc.sbuf_tensor(name, list(shape), dt))
        ps = lambda name, shape, dt=F32: st.enter_context(nc.psum_tensor(name, list(shape), dt))
        BC = sb("BC", [128, NH * 9 * 128], BF16)
        identb = sb("identb3", [128, 128], BF16)
        onesb = sb("onesb", [128, 64], BF16)
        w_ao = sb("w_ao", [128, 4 * D], BF16)
        w_out = sb("w_out", [128, 8 * D], BF16)
        lnp = sb("lnp3", [128, 2 * D], F32)
        bcm_sb = sb("bcm_sb", [128, 9 * 128], F32)
        stage = [sb(f"stg3{i}", [128, 2048], F32) for i in range(2)]
        q_in2 = [sb(f"q_in{i}", [96, NH * TT], BF16) for i in range(2)]
        k_in2 = [sb(f"k_in{i}", [96, NH * 1024], BF16) for i in range(2)]
        v_in2 = [sb(f"v_in{i}", [128, 8 * 512], BF16) for i in range(2)]
        o_cp = sb("o_cp", [128, TT], F32)
        s_cp = sb("s_cp", [128, TT], F32)
        G1_in = sb("G1_in", [128, 8 * TT], BF16)
        ga_in = sb("ga_in", [128, 8 * TT], BF16)
        pT = [sb(f"pT{i}", [128, 1024], BF16) for i in range(2)]
        rs_sb = sb("rs", [128, TT], F32)
        aT2 = [sb(f"aT{i}", [128, 4 * TT], BF16) for i in range(2)]
        tmpg = [sb(f"tmpg{i}", [128, TT], F32) for i in range(2)]
        mixed = sb("mixed", [128, 8 * TT], BF16)
        s_sb = [sb(f"s_sb{i}", [128, D], F32) for i in range(2)]
        x1_sb = [sb(f"x1_sb{i}", [128, D], F32) for i in range(2)]
        stt = [sb(f"stt{i}", [128, 12], F32) for i in range(2)]
        mv = [sb(f"mv{i}", [128, 4], F32) for i in range(2)]
        psc = [ps(f"psc{i}", [128, 1024]) for i in range(2)]
        pmm = [ps(f"pm3{i}", [128, 512]) for i in range(2)]
        po_o = ps("po_o", [128, 512])
        po_s = ps("po_s", [128, 512])

        with contextlib.ExitStack() as st2:
            S = Sch(nc, st2, "w3", ["wl", "st0", "st1", "m0", "m1"])
            Rstage = [Reg(), Reg()]
            load_cast(S, w_ao, w_ao_r, stage, Rstage, Reg(), 4 * D)
            load_cast(S, w_out, w_out_r, stage, Rstage, Reg(), 8 * D)
            load_cast(S, identb, ident_r, stage, Rstage, Reg(), 128)
            S.dma("sp", DMA(lnp[:], lnp_r[:, 0:2 * D]), "m0", W=[Reg()])
            S.op("pool", lambda e: e.memset(onesb[:], 1.0), W=[Reg()])
            RBC = Reg()
            Rbcm = Reg()
            S.dma("sp", DMA(bcm_sb[:], bcm_r), "m1", W=[Rbcm])
            for h in range(NH):
                s = h % 2
                S.dma("sp", DMA(stage[s][:, 0:1152], rpbx_r[:, h * 1152:(h + 1) * 1152]), f"st{s}", W=[Rstage[s]])
                S.op("dve", TTOP(BC[:, h * 1152:(h + 1) * 1152], stage[s][:, 0:1152], bcm_sb[:], ALU.add),
                     R=[Rstage[s], Rbcm], Wacc=[RBC])
            S.finish()
            S.replay()

        with contextlib.ExitStack() as st2:
            dsems = ["qi0", "ki0", "vi0", "qi1", "ki1", "vi1", "g1i", "gai", "xi0", "xi1", "xo0", "xo1"]
            S = Sch(nc, st2, "p3", dsems)
            Rq2, Rk2, Rv2 = [Reg(), Reg()], [Reg(), Reg()], [Reg(), Reg()]
            Rg1, Rga = Reg(), Reg()
            Rocp, Rscp = Reg(), Reg()
            Rxi = [Reg(), Reg()]
            Rxo = [Reg(), Reg()]
            Rpsc = [Reg(), Reg()]
            Rpo = Reg()
            Rpm = [Reg(), Reg()]
            RpT = [Reg(), Reg()]
            Rrs = Reg()
            RaT2 = [Reg(), Reg()]
            Rmixed = Reg()
            Rtmpg = [Reg(), Reg()]
            Rs = [Reg(), Reg()]
            Rstt = [Reg(), Reg()]
            Rmv = [Reg(), Reg()]
            BC3 = BC[:].rearrange("p (h x) -> p h x", h=NH)
            wao3 = w_ao[:].rearrange("p (c n) -> p c n", c=4)
            wout3 = w_out[:].rearrange("p (k n) -> p k n", k=8)
            g1i3 = G1_in[:].rearrange("p (c t) -> p c t", c=8)
            gai3 = ga_in[:].rearrange("p (c t) -> p c t", c=8)
            mx3 = mixed[:].rearrange("p (c t) -> p c t", c=8)
            cnt = {"pm": 0, "blk": 0, "step": 0}

            def att_specs(first, last):
                steps = [[(3, 0, 3), (4, 0, 3)],
                         [(2, 0, 2), (0, 0, 0), (5, 1, 3), (7, 3, 3)],
                         [(1, 0, 1), (6, 2, 3)]]
                if first:
                    steps[2] = steps[2] + [(5, 0, 0), (6, 0, 0)]
                if last:
                    steps[2] = steps[2] + [(2, 3, 3), (1, 3, 3)]
                out = []
                for L in steps:
                    off = 0
                    sp = []
                    for (kt, lo, hi) in L:
                        w = (hi - lo + 1) * 128
                        assert off // 512 == (off + w - 1) // 512 and off + w <= 1024
                        sp.append((kt, lo, hi, off, w))
                        off += w
                    out.append((sp, off))
                return out

            def views(tj):
                sj = tj % 2
                return (q_in2[sj][:].rearrange("p (h t) -> p h t", h=NH),
                        k_in2[sj][:].rearrange("p (h t) -> p h t", h=NH),
                        v_in2[sj][:].rearrange("p (b f) -> p b f", b=8))

            def issue_qkv(tj):
                e0j = real_tiles[tj][0]
                kwj = e0j - 256
                sj = tj % 2
                q3j, k3j, v3j = views(tj)
                S.dma("sp", DMA(q3j[0:64, :, :], qT_d.rearrange("(h d) t -> d h t", d=64)[:, :, e0j:e0j + TT]), f"qi{sj}", W=[Rq2[sj]])
                for h in range(NH):
                    S.dma("sp", DMA(q3j[64:96, h, :], qm_r[:, e0j:e0j + TT]), f"qi{sj}", Wacc=[Rq2[sj]])
                S.dma("sp", DMA(k3j[0:64, :, :], kT_d.rearrange("(h d) t -> d h t", d=64)[:, :, kwj:kwj + 1024]), f"ki{sj}", W=[Rk2[sj]])
                for h in range(NH):
                    S.dma("sp", DMA(k3j[64:96, h, :], km_r[:, kwj:kwj + 1024]), f"ki{sj}", Wacc=[Rk2[sj]])
                S.dma("sp", DMA(v3j, v_d[kwj:kwj + 1024, :].rearrange("(b p) f -> p b f", p=128)), f"vi{sj}", W=[Rv2[sj]])

            def issue_gates(tj):
                e0j = real_tiles[tj][0]
                S.dma("sp", DMA(g1i3, fm(G1_d, e0j, TT)), "g1i", W=[Rg1])
                S.dma("sp", DMA(gai3, fm(ga_d, e0j, TT)), "gai", W=[Rga])

            all_steps = []
            for tj, (_e, _o, f_, l_) in enumerate(real_tiles):
                sp3 = att_specs(f_, l_)
                for h in range(NH):
                    for si in range(3):
                        all_steps.append((tj, h, si, sp3[si]))

            def emit_scores(gi):
                tj, h, si, (sp, tot) = all_steps[gi]
                q3, k3, v3 = views(tj)
                b = gi % 2
                mms = []
                for (kt, lo, hi, off, w) in sp:
                    o = psc[b][:, off:off + w]
                    mms.append((off // 512, o, k3[0:96, h, kt * 128:(kt + 1) * 128], q3[0:96, h, lo * 128:(hi + 1) * 128]))
                    d0 = 6 - kt + lo
                    mms.append((off // 512, o, identb[:], BC3[:, h, d0 * 128:d0 * 128 + w]))
                fns = []
                for i, (bk, o, l, r_) in enumerate(mms):
                    st_ = all(m[0] != bk for m in mms[:i])
                    sp_ = all(m[0] != bk for m in mms[i + 1:])
                    fns.append(MM(o, l, r_, st_, sp_))
                S.multi("pe", fns, R=[Rq2[tj % 2], Rk2[tj % 2]], W=[Rpsc[b]])

            def emit_pv(gi):
                tj, h, si, (sp, tot) = all_steps[gi]
                q3, k3, v3 = views(tj)
                b = gi % 2
                p0 = (h % 2) * 64
                pr = h // 2
                if h % 2 == 0 and si == 0:
                    Rpo.newgen()
                S.op("act", ACT(pT[b][:, 0:tot], psc[b][:, 0:tot], AF.Exp), R=[Rpsc[b]], W=[RpT[b]])
                fns = []
                nsp = len(sp)
                for dst, lhs_of in ((po_o, lambda kt: v3[:, kt, h * 64:(h + 1) * 64]), (po_s, lambda kt: onesb[:, 0:64])):
                    for i, (kt, lo, hi, off, w) in enumerate(sp):
                        fns.append(MM(dst[p0:p0 + 64, lo * 128:(hi + 1) * 128], lhs_of(kt), pT[b][:, off:off + w],
                                      si == 0 and i == 0, si == 2 and i == nsp - 1))
                S.multi("pe", fns, R=[RpT[b], Rv2[tj % 2]], Wacc=[Rpo])
                if h % 2 == 1 and si == 2:
                    a3 = aT2[tj % 2][:].rearrange("p (c t) -> p c t", c=4)
                    if pr == 0:
                        RaT2[tj % 2].newgen()
                    S.op("act", ACT(s_cp[:], po_s[:, :], AF.Copy), R=[Rpo], W=[Rscp])
                    S.op("act", ACT(o_cp[:], po_o[:, :], AF.Copy), R=[Rpo], W=[Rocp])
                    S.op("dve", lambda e: e.reciprocal(out=rs_sb[:], in_=s_cp[:]), R=[Rscp], W=[Rrs])
                    S.op("dve", TTOP(a3[:, pr, :], o_cp[:], rs_sb[:], ALU.mult), R=[Rocp, Rrs], Wacc=[RaT2[tj % 2]])

            def tail_ops(tj):
                e0 = real_tiles[tj][0]
                a3 = aT2[tj % 2][:].rearrange("p (c t) -> p c t", c=4)
                RaT = RaT2[tj % 2]
                ops = []

                def att_out(nn):
                    if nn == 0:
                        Rmixed.newgen()
                    pi = cnt["pm"] % 2
                    cnt["pm"] += 1
                    S.multi("pe", [MM(pmm[pi][:, 0:TT], wao3[:, c_, nn * 128:(nn + 1) * 128], a3[:, c_, :], c_ == 0, c_ == 3)
                                   for c_ in range(4)], R=[RaT], W=[Rpm[pi]])
                    S.op("dve", TTOP(tmpg[pi][:], pmm[pi][:, 0:TT], gai3[:, nn, :], ALU.mult), R=[Rpm[pi], Rga], W=[Rtmpg[pi]])
                    S.op("pool", TTOP(mx3[:, nn, :], tmpg[pi][:], g1i3[:, nn, :], ALU.add), R=[Rtmpg[pi], Rg1], Wacc=[Rmixed])
                    if nn == 7 and tj + 1 < len(real_tiles):
                        issue_gates(tj + 1)

                def w_out_half(b, half):
                    if half == 0:
                        cnt["blk"] += 1
                    bi = (cnt["blk"] - 1) % 2
                    if half == 0:
                        S.dma("sp", DMA(stage[bi][:, 0:D], x_ext[e0 + b * 128:e0 + (b + 1) * 128, :]), f"xi{bi}", W=[Rxi[bi]])
                    pi = cnt["pm"] % 2
                    cnt["pm"] += 1
                    S.multi("pe", [MM(pmm[pi][:, 0:512], mx3[:, kc, b * 128:(b + 1) * 128], wout3[:, kc, half * 512:(half + 1) * 512], kc == 0, kc == 7)
                                   for kc in range(8)], R=[Rmixed], W=[Rpm[pi]])
                    S.op("dve", STT(s_sb[bi][:, half * 512:(half + 1) * 512], stage[bi][:, half * 512:(half + 1) * 512], ALPHA,
                                    pmm[pi][:, 0:512], ALU.mult, ALU.add), R=[Rxi[bi], Rpm[pi]],
                         **({"W": [Rs[bi]]} if half == 0 else {"Wacc": [Rs[bi]]}))
                    S.op("dve", lambda e, bi=bi, half=half: e.bn_stats(out=stt[bi][:, half * 6:(half + 1) * 6],
                                                                      in_=s_sb[bi][:, half * 512:(half + 1) * 512]),
                         R=[Rs[bi]], **({"W": [Rstt[bi]]} if half == 0 else {"Wacc": [Rstt[bi]]}))
                    if half == 1:
                        def store(bi=bi, b=b):
                            S.dma("pool", DMA(x1_d[e0 + b * 128:e0 + (b + 1) * 128, :], x1_sb[bi][:]), f"xo{bi}", R=[Rxo[bi]])
                        stg = layer_norm_stages(S, s_sb[bi], stt[bi], mv[bi], x1_sb[bi], lnp[:, 0:D], lnp[:, D:2 * D],
                                                Rs[bi], Rstt[bi], Rmv[bi], Rxo[bi], after=store, lnexp=True)
                        stg[0]()
                        later.append([1, stg[1]])
                        later.append([2, stg[2]])

                for nn in range(8):
                    ops.append(lambda nn=nn: att_out(nn))
                for b in range(4):
                    for half in range(2):
                        ops.append(lambda b=b, half=half: w_out_half(b, half))
                return ops

            later = []

            def run_later(flush=False):
                keep = []
                for item in later:
                    item[0] -= 1
                    if item[0] <= 0 or flush:
                        item[1]()
                    else:
                        keep.append(item)
                later[:] = keep

            NT = len(real_tiles)
            issue_qkv(0)
            issue_gates(0)
            emit_scores(0)
            pending = []
            for gi in range(len(all_steps)):
                tj, h, si, _sp = all_steps[gi]
                if h == 0 and si == 0 and tj + 1 < NT:
                    issue_qkv(tj + 1)
                if gi + 1 < len(all_steps):
                    emit_scores(gi + 1)
                emit_pv(gi)
                run_later()
                if pending and (gi % 3 != 0):
                    pending.pop(0)()
                if h == NH - 1 and si == 2:
                    while pending:
                        pending.pop(0)()
                        run_later()
                    pending = tail_ops(tj)
            while pending:
                pending.pop(0)()
                run_later()
            run_later(flush=True)
            run_later(flush=True)
            S.finish()
            S.replay()

    with contextlib.ExitStack() as st:
        sb = lambda name, shape, dt: st.enter_context(nc.sbuf_tensor(name, list(shape), dt))
        ps = lambda name, shape, dt=F32: st.enter_context(nc.psum_tensor(name, list(shape), dt))
        T4 = 256
        w_up = sb("w_up", [128, 8 * DFF], BF16)
        w_dn = sb("w_dn", [128, 32 * D], BF16)
        ident = sb("ident4", [128, 128], BF16)
        x1b = [sb(f"x1b{i}", [128, D], BF16) for i in range(2)]
        lnp = sb("lnp4", [128, 2 * D], F32)
        x1_in = [sb(f"x1_in{i}", [128, D], F32) for i in range(4)]
        x1T = sb("x1T", [128, 8 * T4], BF16)
        hT = sb("hT", [128, 32 * T4], BF16)
        rl = [sb(f"rl{i}", [128, T4], F32) for i in range(2)]
        s_sb = [sb(f"s4_{i}", [128, D], F32) for i in range(2)]
        o_sb = [sb(f"o4_{i}", [128, D], F32) for i in range(2)]
        stt = [sb(f"stt4{i}", [128, 12], F32) for i in range(2)]
        mv = [sb(f"mv4{i}", [128, 4], F32) for i in range(2)]
        ptr = [ps(f"ptr4{i}", [128, 1024], BF16) for i in range(2)]
        pup = [ps(f"pup{i}", [128, 512]) for i in range(2)]
        pdn = [ps(f"pdn{i}", [128, 512]) for i in range(4)]

        with contextlib.ExitStack() as st2:
            S = Sch(nc, st2, "w4", ["wl", "st0", "st1", "m0", "m1"])
            stage = [s_sb[0], s_sb[1]]
            Rstage = [Reg(), Reg()]
            load_cast(S, w_up, w_up_r, stage, Rstage, Reg(), 8 * DFF, chunk=1024)
            load_cast(S, w_dn, w_dn_r, stage, Rstage, Reg(), 32 * D, chunk=1024)
            load_cast(S, ident, ident_r, None, None, Reg(), 128)
            S.dma("sp", DMA(lnp[:], lnp_r[:, 2 * D:4 * D]), "m1", W=[Reg()])
            S.finish()
            S.replay()

        with contextlib.ExitStack() as st2:
            dsems = [f"xi{i}" for i in range(4)] + ["oo0", "oo1"]
            S = Sch(nc, st2, "p4", dsems)
            Rxi = [Reg() for _ in range(4)]
            Rptr = [Reg(), Reg()]
            Rpup = [Reg(), Reg()]
            Rpdn = [Reg() for _ in range(4)]
            RxT, RhT = Reg(), Reg()
            Rrl = [Reg(), Reg()]
            Rs = [Reg(), Reg()]
            Ro = [Reg(), Reg()]
            Rstt = [Reg(), Reg()]
            Rmv = [Reg(), Reg()]
            xT3 = x1T[:].rearrange("p (k t) -> p k t", k=8)
            hT3 = hT[:].rearrange("p (f t) -> p f t", f=32)
            wup3 = w_up[:].rearrange("p (k n) -> p k n", k=8)
            wdn3 = w_dn[:].rearrange("p (f n) -> p f n", f=32)
            cnt = {"pt": 0, "pu": 0, "pd": 0, "blk": 0, "g": 0}
            subs = []
            for (e0, o0_, _f, _l) in real_tiles:
                for sub in range(TT // T4):
                    subs.append((e0 + sub * T4, o0_ + sub * T4))
            slots_of = {}

            def emit_load(k):
                es, _ = subs[k]
                sl_ = []
                for b in range(2):
                    s_ = cnt["g"] % 4
                    cnt["g"] += 1
                    sl_.append(s_)
                    S.dma("sp", DMA(x1_in[s_][:], x1_d[es + b * 128:es + (b + 1) * 128, :]), f"xi{s_}", W=[Rxi[s_]])
                slots_of[k] = sl_

            Rx1b = [Reg(), Reg()]

            def emit_transposes(k):
                RxT.newgen()
                for b in range(2):
                    s_ = slots_of[k][b]
                    S.op("dve", COPY(x1b[b][:], x1_in[s_][:]), R=[Rxi[s_]], W=[Rx1b[b]])
                    pi = cnt["pt"] % 2
                    cnt["pt"] += 1
                    S.multi("pe", [TR(ptr[pi][:, i * 128:(i + 1) * 128], x1b[b][:, i * 128:(i + 1) * 128], ident[:])
                                   for i in range(8)], R=[Rx1b[b]], W=[Rptr[pi]])
                    src = ptr[pi][:].rearrange("p (k t) -> p k t", k=8)
                    dst = xT3[:, :, b * 128:(b + 1) * 128]
                    if b == 0:
                        S.op("act", ACT(dst, src, AF.Copy), R=[Rptr[pi]], Wacc=[RxT])
                    else:
                        S.op("dve", COPY(dst, src), R=[Rptr[pi]], Wacc=[RxT])

            later4 = {}

            def emit_up(k):
                RhT.newgen()
                for f in range(32):
                    pi = cnt["pu"] % 2
                    cnt["pu"] += 1
                    S.multi("pe", [MM(pup[pi][:, 0:T4], wup3[:, kc, f * 128:(f + 1) * 128], xT3[:, kc, :], kc == 0, kc == 7)
                                   for kc in range(8)], R=[RxT], W=[Rpup[pi]])
                    S.op("act", ACT(rl[pi][:], pup[pi][:, 0:T4], AF.Relu), R=[Rpup[pi]], W=[Rrl[pi]])
                    S.op("dve", TTOP(hT3[:, f, :], rl[pi][:], rl[pi][:], ALU.mult), R=[Rrl[pi]], Wacc=[RhT])
                    for fn in later4.pop(f, []):
                        fn()

            def emit_down(k, last_sub):
                _, os_ = subs[k]
                for b in range(2):
                    s_ = slots_of[k][b]
                    bi = cnt["blk"] % 2
                    cnt["blk"] += 1
                    for half in range(2):
                        pi = cnt["pd"] % 4
                        cnt["pd"] += 1
                        S.multi("pe", [MM(pdn[pi][:, 0:512], hT3[:, f, b * 128:(b + 1) * 128], wdn3[:, f, half * 512:(half + 1) * 512], f == 0, f == 31)
                                       for f in range(32)], R=[RhT], W=[Rpdn[pi]])
                        S.op("dve", STT(s_sb[bi][:, half * 512:(half + 1) * 512], x1_in[s_][:, half * 512:(half + 1) * 512], ALPHA,
                                        pdn[pi][:, 0:512], ALU.mult, ALU.add), R=[Rxi[s_], Rpdn[pi]],
                             **({"W": [Rs[bi]]} if half == 0 else {"Wacc": [Rs[bi]]}))
                        S.op("dve", lambda e, bi=bi, half=half: e.bn_stats(out=stt[bi][:, half * 6:(half + 1) * 6],
                                                                          in_=s_sb[bi][:, half * 512:(half + 1) * 512]),
                             R=[Rs[bi]], **({"W": [Rstt[bi]]} if half == 0 else {"Wacc": [Rstt[bi]]}))

                    def store(bi=bi, b=b):
                        S.dma("pool", DMA(out_d[os_ + b * 128:os_ + (b + 1) * 128, :], o_sb[bi][:]), f"oo{bi}", R=[Ro[bi]])
                    stg = layer_norm_stages(S, s_sb[bi], stt[bi], mv[bi], o_sb[bi], lnp[:, 0:D], lnp[:, D:2 * D],
                                            Rs[bi], Rstt[bi], Rmv[bi], Ro[bi], after=store)
                    stg[0]()
                    if b == 0 or last_sub:
                        stg[1]()
                        stg[2]()
                    else:
                        later4.setdefault(3, []).append(stg[1])
                        later4.setdefault(6, []).append(stg[2])

            nsub = len(subs)
            emit_load(0)
            emit_transposes(0)
            for k in range(nsub):
                if k + 1 < nsub:
                    emit_load(k + 1)
                emit_up(k)
                if k + 1 < nsub:
                    emit_transposes(k + 1)
                emit_down(k, k == nsub - 1)
            S.finish()
            S.replay()
    return nc


def layer_norm_stages(S, s, stt, mv, out, g, b, Rs, Rstt, Rmv, Rout, after=None, lnexp=False):
    def st1():
        S.op("dve", lambda e: e.bn_aggr(out=mv[:, 0:2], in_=stt[:, 0:12]), R=[Rstt], W=[Rmv])
        S.op("dve", TSOP(mv[:, 2:3], mv[:, 1:2], EPS, None, ALU.add, ALU.bypass), W=[Rmv])

    def st2():
        if lnexp:
            S.op("act", ACT(mv[:, 2:3], mv[:, 2:3], AF.Ln), W=[Rmv])
            S.op("act", ACT(mv[:, 2:3], mv[:, 2:3], AF.Exp, scale=-0.5), W=[Rmv])
        else:
            S.op("act", ACT(mv[:, 2:3], mv[:, 2:3], AF.Sqrt), W=[Rmv])
            S.op("dve", lambda e: e.reciprocal(out=mv[:, 2:3], in_=mv[:, 2:3]), W=[Rmv])
        S.op("dve", STT(mv[:, 3:4], mv[:, 0:1], -1.0, mv[:, 2:3], ALU.mult, ALU.mult), W=[Rmv])

    def st3():
        S.op("act", ACT(s[:], s[:], AF.Identity, bias=mv[:, 3:4], scale=mv[:, 2:3]), R=[Rmv], W=[Rs])
        S.op("pool", TTOP(s[:], s[:], g, ALU.mult), W=[Rs])
        S.op("pool", TTOP(out[:], s[:], b, ALU.add), R=[Rs], W=[Rout])
        if after is not None:
            after()
    return [st1, st2, st3]


def layer_norm_tail(S, s, stt, mv, out, g, b, Rs, Rstt, Rmv, Rout):
    for f in layer_norm_stages(S, s, stt, mv, out, g, b, Rs, Rstt, Rmv, Rout):
        f()


def _bc_tables(rpb):
    a = np.arange(2)[:, None, None, None, None]
    kc = np.arange(64)[None, :, None, None, None]
    dl = np.arange(4, -5, -1)[None, None, :, None, None]
    b = np.arange(2)[None, None, None, :, None]
    qc = np.arange(64)[None, None, None, None, :]
    dr = 2 * dl + a - b + 0 * kc + 0 * qc
    cs = np.clip(qc - 8, 0, 48)
    valid = (np.abs(dr) <= 7) & (kc >= cs) & (kc < cs + 16)
    dri = np.clip(dr + 7, 0, 14)
    dci = np.clip(kc - qc + 15, 0, 30) + 0 * dr
    dri, dci, valid = np.broadcast_arrays(dri, dci, valid)
    g = rpb[:, dri, dci]
    g = g.transpose(1, 2, 0, 3, 4, 5).reshape(128, NH * 9 * 128)
    m = np.where(valid, np.float32(0.0), np.float32(NEG)).astype(np.float32).reshape(128, 9 * 128)
    return np.ascontiguousarray(g.astype(np.float32)), np.ascontiguousarray(m)


def _row_tables(segs_info, TEXT):
    km = np.zeros((32, TEXT), np.float32)
    qm = np.zeros((32, TEXT), np.float32)
    for (so, s_, roff, rseq) in segs_info:
        e = np.arange(so, so + s_ + 2 * HALO)
        rl = np.floor_divide(e - so - HALO, GRID_W)
        rg = rl + roff
        km[np.mod(rg, 32), e] = 1.0
        rs = np.clip(rg - 4, 0, rseq - 8)
        res = np.arange(32)[:, None]
        kr = rg[None, :] - 15 + np.mod(res - (rg[None, :] - 15), 32)
        ok = (kr >= rs[None, :]) & (kr < rs[None, :] + 8)
        qm[:, e] = np.where(ok, 0.0, NEG)
    return km.astype(ml_dtypes.bfloat16), qm.astype(ml_dtypes.bfloat16)


def _host_common(w_in, dw_kernel, dw_bias, conv_ln_g, conv_ln_b, w_conv_out, rpb, w_att_out,
                 w_out, ln1_g, ln1_b, w_up, w_down, ln2_g, ln2_b):
    f = np.float32
    r = lambda w, k: np.ascontiguousarray(np.asarray(w, f).reshape(k, 128, -1).transpose(1, 0, 2).reshape(128, -1))
    com = {}
    com["w_in_r"] = r(w_in[0], 8)
    com["w_co_r"] = r(w_conv_out[0], 4)
    com["w_ao_r"] = r(w_att_out[0], 4)
    com["w_out_r"] = r(w_out[0], 8)
    com["w_up_r"] = r(w_up[0], 8)
    com["w_dn_r"] = r(w_down[0], 32)
    com["dwk_r"] = np.ascontiguousarray(np.asarray(dw_kernel[0], f).reshape(CK, 4, 128).transpose(2, 1, 0).reshape(128, 4 * CK))
    cp = np.stack([np.asarray(v[0], f).reshape(4, 128).T for v in (dw_bias, conv_ln_g, conv_ln_b)], axis=1)
    com["cpar_r"] = np.ascontiguousarray(cp.reshape(128, 12))
    lnp = np.concatenate([np.asarray(v[0], f) for v in (ln1_g, ln1_b, ln2_g, ln2_b)])[None, :]
    com["lnp_r"] = np.ascontiguousarray(np.broadcast_to(lnp, (128, 4 * D)))
    com["ident_r"] = np.eye(128, dtype=f)
    g, m = _bc_tables(np.asarray(rpb[0], f))
    com["rpbx_r"] = g
    com["bcm_r"] = m
    return com


def _prepare(x_prompt, x_sample, w_in, dw_kernel, dw_bias, conv_ln_g, conv_ln_b, w_conv_out, rpb,
             w_att_out, w_out, ln1_g, ln1_b, w_up, w_down, ln2_g, ln2_b):
    x_prompt = np.asarray(x_prompt, np.float32)
    x_sample = np.asarray(x_sample, np.float32)
    NB, SEQ, _ = x_prompt.shape
    DB, DSEQ, _ = x_sample.shape
    ncores = DB
    parts = ncores // NB
    SB = SEQ // parts
    SA = DSEQ
    com = _host_common(w_in, dw_kernel, dw_bias, conv_ln_g, conv_ln_b, w_conv_out, rpb, w_att_out,
                       w_out, ln1_g, ln1_b, w_up, w_down, ln2_g, ln2_b)
    EA, EB = SA + 2 * HALO, SB + 2 * HALO
    TEXT = EA + EB
    in_maps = []
    for c in range(ncores):
        pi, qi = c // parts, c % parts
        xe = np.zeros((TEXT, D), np.float32)
        xe[HALO:HALO + SA] = x_sample[c]
        lo, hi = qi * SB - HALO, (qi + 1) * SB + HALO
        slo, shi = max(lo, 0), min(hi, SEQ)
        xe[EA + (slo - lo):EA + (slo - lo) + (shi - slo)] = x_prompt[pi, slo:shi]
        km, qm = _row_tables([(0, SA, 0, SA // GRID_W), (EA, SB, qi * (SB // GRID_W), SEQ // GRID_W)], TEXT)
        m = dict(com)
        m["x_ext"] = xe
        m["km_r"] = km
        m["qm_r"] = qm
        in_maps.append(m)

    def assemble(results):
        y_prompt = np.zeros_like(x_prompt)
        y_sample = np.zeros_like(x_sample)
        for c in range(ncores):
            if results[c] is None:
                continue
            o = np.asarray(results[c]["out"], np.float32)
            pi, qi = c // parts, c % parts
            y_sample[c] = o[0:SA]
            y_prompt[pi, qi * SB:(qi + 1) * SB] = o[SA:SA + SB]
        return (y_prompt, y_sample)
    return SA, SB, in_maps, assemble


def kernel(**inputs):
    SA, SB, in_maps, assemble = _prepare(**inputs)
    nc = build_program(SA, SB)
    res = run_bass_kernel_spmd(nc, in_maps, core_ids=list(range(len(in_maps))))
    return assemble(res.results)
```

```python
import contextlib
import numpy as np
import ml_dtypes
import concourse.bass as bass
import concourse.mybir as mybir
from concourse.bass_utils import run_bass_kernel_spmd

F32 = mybir.dt.float32
BF16 = mybir.dt.bfloat16
AF = mybir.ActivationFunctionType
ALU = mybir.AluOpType

D = 1024
NIN = 4608
DFF = 4096
CK = 31
NH = 8
TT = 512
HALO = 256
ALPHA = float(2.0 ** 0.25)
EPS = 1e-5
NEG = -30000.0
GRID_W = 64

SAME_ENGINE_SYNC = True


class Reg:
    def __init__(self, name=""):
        self.name = name
        self.w = []
        self.r = []
        self.prev = []

    def newgen(self):
        self.prev = self.w + self.r
        self.w = []
        self.r = []


class Sch:
    ENG = ["pe", "act", "dve", "pool", "sp"]

    def __init__(self, nc, stack, tag, dma_sems):
        self.nc = nc
        self.q = {e: [] for e in self.ENG}
        self.sem = {}
        self.cnt = {}
        self.waited = {e: {} for e in self.ENG}
        for k in ["pe", "act", "dve", "pool"] + list(dma_sems) + ["bar"]:
            self.sem[k] = stack.enter_context(nc.semaphore(f"{tag}_{k}"))
            self.cnt[k] = 0

    def _waits(self, eng, deps):
        need = {}
        for (k, v) in deps:
            if k == "pe" and eng == "pe":
                continue
            if (not SAME_ENGINE_SYNC) and k == eng:
                continue
            if v > need.get(k, 0):
                need[k] = v
        out = []
        for k, v in need.items():
            if self.waited[eng].get(k, 0) < v:
                self.waited[eng][k] = v
                out.append((k, v))
        return out

    @staticmethod
    def _deps(R, W, Wacc):
        deps = []
        for r in R:
            deps += r.w
        for r in W:
            deps += r.w + r.r + r.prev
        for r in Wacc:
            deps += r.prev
        return deps

    @staticmethod
    def _commit(t, R, W, Wacc):
        for r in R:
            r.r.append(t)
        for r in W:
            r.w = [t]
            r.r = []
            r.prev = []
        for r in Wacc:
            r.w.append(t)

    def op(self, eng, fn, R=(), W=(), Wacc=()):
        return self.multi(eng, [fn], R, W, Wacc)

    def multi(self, eng, fns, R=(), W=(), Wacc=()):
        waits = self._waits(eng, self._deps(R, W, Wacc))
        n = len(fns)
        self.cnt[eng] += 1
        t = (eng, self.cnt[eng])
        for i, fn in enumerate(fns):
            self.q[eng].append((waits if i == 0 else [], fn, (eng, 1) if i == n - 1 else None))
        self._commit(t, R, W, Wacc)
        return t

    def dma(self, eng, fn, sem, R=(), W=(), Wacc=()):
        waits = self._waits(eng, self._deps(R, W, Wacc))
        self.cnt[sem] += 16
        t = (sem, self.cnt[sem])
        self.q[eng].append((waits, fn, (sem, 16)))
        self._commit(t, R, W, Wacc)
        return t

    def finish(self):
        deps = [(k, v) for k, v in self.cnt.items() if v > 0 and k != "bar"]
        waits = self._waits("sp", deps)
        self.q["sp"].append((waits, lambda e: e.nop(), ("bar", 1)))
        for e in ["pe", "act", "dve", "pool", "sp"]:
            self.q[e].append(([("bar", 1)], None, None))

    def replay(self):
        with self.nc.Block() as blk:
            def run(key):
                def body(e):
                    for (waits, fn, sig) in self.q[key]:
                        for (k, v) in waits:
                            e.wait_ge(self.sem[k], v)
                        if fn is None:
                            continue
                        ins = fn(e)
                        if sig is not None:
                            ins.then_inc(self.sem[sig[0]], sig[1])
                return body
            blk.tensor(run("pe"))
            blk.scalar(run("act"))
            blk.vector(run("dve"))
            blk.gpsimd(run("pool"))
            blk.sync(run("sp"))


def MM(out, lhsT, rhs, start, stop):
    return lambda e: e.matmul(out, lhsT=lhsT, rhs=rhs, start=start, stop=stop)


def TR(out, in_, ident):
    return lambda e: e.transpose(out=out, in_=in_, identity=ident)


def ACT(out, in_, func, bias=None, scale=None):
    def f(e):
        kw = {}
        if bias is not None:
            kw["bias"] = bias
        if scale is not None:
            kw["scale"] = scale
        return e.activation(out=out, in_=in_, func=func, **kw)
    return f


def COPY(out, in_):
    return lambda e: e.tensor_copy(out=out, in_=in_)


def TTOP(out, a, b, op):
    return lambda e: e.tensor_tensor(out=out, in0=a, in1=b, op=op)


def TSOP(out, a, s1, s2, op0, op1):
    return lambda e: e.tensor_scalar(out=out, in0=a, scalar1=s1, scalar2=s2, op0=op0, op1=op1)


def STT(out, a, s, b, op0, op1):
    return lambda e: e.scalar_tensor_tensor(out=out, in0=a, scalar=s, in1=b, op0=op0, op1=op1)


def DMA(out, in_):
    return lambda e: e.dma_start(out=out, in_=in_)


def load_cast(S, dst3, src_dram2, stage, Rstage, Rdst, ncols, rows=128, chunk=4096):
    i = 0
    k = 0
    for c0 in range(0, ncols, chunk):
        cw = min(chunk, ncols - c0)
        if stage is None or i % 2 == 0:
            S.dma("pool", DMA(dst3[0:rows, c0:c0 + cw], src_dram2[0:rows, c0:c0 + cw]), "wl", Wacc=[Rdst])
        else:
            sw = stage[0].shape[-1]
            for d0 in range(0, cw, sw):
                dw = min(sw, cw - d0)
                s_ = k % len(stage)
                S.dma("sp", DMA(stage[s_][0:rows, 0:dw], src_dram2[0:rows, c0 + d0:c0 + d0 + dw]), f"st{s_}", W=[Rstage[s_]])
                if k % 2 == 0:
                    fn = COPY(dst3[0:rows, c0 + d0:c0 + d0 + dw], stage[s_][0:rows, 0:dw])
                    S.op("dve", fn, R=[Rstage[s_]], Wacc=[Rdst])
                else:
                    fn = ACT(dst3[0:rows, c0 + d0:c0 + d0 + dw], stage[s_][0:rows, 0:dw], AF.Copy)
                    S.op("act", fn, R=[Rstage[s_]], Wacc=[Rdst])
                k += 1
        i += 1


def build_program(SA, SB):
    EA = SA + 2 * HALO
    EB = SB + 2 * HALO
    TEXT = EA + EB
    TREAL = SA + SB
    segs = [(0, SA), (EA, SB)]
    real_tiles = []
    o0 = 0
    for (so, s_) in segs:
        nt = s_ // TT
        for i in range(nt):
            real_tiles.append((so + HALO + i * TT, o0 + i * TT, i == 0, i == nt - 1))
        o0 += s_

    nc = bass.Bass("TRN2", target_bir_lowering=False)

    def din(name, shape, dt=F32):
        return nc.dram_tensor(name, list(shape), dt, kind="ExternalInput").ap()

    def dscr(name, shape, dt):
        return nc.dram_tensor(name, list(shape), dt, kind="Internal").ap()

    x_ext = din("x_ext", [TEXT, D])
    w_in_r = din("w_in_r", [128, 8 * NIN])
    w_co_r = din("w_co_r", [128, 4 * D])
    w_ao_r = din("w_ao_r", [128, 4 * D])
    w_out_r = din("w_out_r", [128, 8 * D])
    w_up_r = din("w_up_r", [128, 8 * DFF])
    w_dn_r = din("w_dn_r", [128, 32 * D])
    dwk_r = din("dwk_r", [128, 4 * CK])
    cpar_r = din("cpar_r", [128, 12])
    lnp_r = din("lnp_r", [128, 4 * D])
    ident_r = din("ident_r", [128, 128])
    rpbx_r = din("rpbx_r", [128, NH * 9 * 128])
    bcm_r = din("bcm_r", [128, 9 * 128])
    km_r = din("km_r", [32, TEXT], BF16)
    qm_r = din("qm_r", [32, TEXT], BF16)
    out_d = nc.dram_tensor("out", [TREAL, D], F32, kind="ExternalOutput").ap()

    cT_d = dscr("cT_d", [512, TEXT], BF16)
    qT_d = dscr("qT_d", [512, TEXT], BF16)
    kT_d = dscr("kT_d", [512, TEXT], BF16)
    v_d = dscr("v_d", [TEXT, 512], BF16)
    gc_d = dscr("gc_d", [D, TEXT], BF16)
    ga_d = dscr("ga_d", [D, TEXT], BF16)
    G1_d = dscr("G1_d", [D, TEXT], BF16)
    x1_d = dscr("x1_d", [TEXT, D], F32)

    def fm(ap, t0, n):
        return ap.rearrange("(c p) t -> p c t", p=128)[:, :, t0:t0 + n]

    with contextlib.ExitStack() as st:
        sb = lambda name, shape, dt: st.enter_context(nc.sbuf_tensor(name, list(shape), dt))
        ps = lambda name, shape, dt=F32: st.enter_context(nc.psum_tensor(name, list(shape), dt))
        w_in = sb("w_in", [128, 8 * NIN], BF16)
        ident = sb("ident1", [128, 128], BF16)
        xs = [sb(f"xs{i}", [128, D], F32) for i in range(4)]
        xb = [sb(f"xb{i}", [128, D], BF16) for i in range(4)]
        xT = [sb(f"xT{i}", [128, 8 * TT], BF16) for i in range(2)]
        c_sb = [sb(f"c_sb{i}", [128, 4 * TT], BF16) for i in range(2)]
        q_sb = [sb(f"q_sb{i}", [128, 4 * TT], BF16) for i in range(2)]
        k_sb = [sb(f"k_sb{i}", [128, 4 * TT], BF16) for i in range(2)]
        v_sb = [sb(f"v_sb{i}", [128, 4 * 512], BF16) for i in range(2)]
        g_sb = [sb(f"g_sb{i}", [128, 16 * TT], BF16) for i in range(2)]
        sg = [sb(f"sg{i}", [128, TT], F32) for i in range(2)]
        ptr = [ps(f"ptr{i}", [128, 1024], BF16) for i in range(2)]
        pmm = [ps(f"pmm{i}", [128, 512]) for i in range(4)]

        with contextlib.ExitStack() as st2:
            S = Sch(nc, st2, "w1", ["wl", "st0", "st1", "st2", "st3"])
            stage = [xs[0], xs[1], xs[2], xs[3]]
            Rstage = [Reg() for _ in range(4)]
            Rw = Reg()
            load_cast(S, w_in, w_in_r, stage, Rstage, Rw, 8 * NIN)
            load_cast(S, ident, ident_r, None, None, Reg(), 128)
            S.finish()
            S.replay()

        with contextlib.ExitStack() as st2:
            dsems = [f"xs{i}" for i in range(4)] + [f"{n}{i}" for n in ["c", "q", "k", "v", "gc", "ga"] for i in range(2)]
            S = Sch(nc, st2, "p1", dsems)
            Rxs = [Reg() for _ in range(4)]
            RxT = [Reg() for _ in range(2)]
            Rptr = [Reg() for _ in range(2)]
            Rpmm = [Reg() for _ in range(4)]
            Rsg = [Reg() for _ in range(2)]
            Rout = {n: [Reg(), Reg()] for n in ["c", "q", "k", "v", "g"]}
            w3 = w_in[:].rearrange("p (k n) -> p k n", k=8)
            state = {"pm": 0, "pt": 0, "sg": 0}

            def nxt_pm():
                i = state["pm"] % 4
                state["pm"] += 1
                return i

            ntile = TEXT // TT
            Rxb = [Reg() for _ in range(4)]
            xT3s = [xT[i][:].rearrange("p (k t) -> p k t", k=8) for i in range(2)]

            def p1_load(j):
                t0 = j * TT
                for b in range(4):
                    s_ = b
                    S.dma("sp", DMA(xs[s_][:], x_ext[t0 + 128 * b: t0 + 128 * b + 128, :]), f"xs{s_}", W=[Rxs[s_]])
                    S.op("dve", COPY(xb[s_][:], xs[s_][:]), R=[Rxs[s_]], W=[Rxb[s_]])

            def p1_T(j):
                sl = j % 2
                RxT[sl].newgen()
                for b in range(4):
                    pi = state["pt"] % 2
                    state["pt"] += 1
                    S.multi("pe", [TR(ptr[pi][:, i * 128:(i + 1) * 128], xb[b][:, i * 128:(i + 1) * 128], ident[:])
                                   for i in range(8)], R=[Rxb[b]], W=[Rptr[pi]])
                    src = ptr[pi][:].rearrange("p (k t) -> p k t", k=8)
                    dst = xT3s[sl][:, :, b * 128:(b + 1) * 128]
                    if b % 2 == 0:
                        S.op("act", ACT(dst, src, AF.Copy), R=[Rptr[pi]], Wacc=[RxT[sl]])
                    else:
                        S.op("dve", COPY(dst, src), R=[Rptr[pi]], Wacc=[RxT[sl]])

            def p1_mm(j, part):
                t0 = j * TT
                sl = j % 2
                xT3 = xT3s[sl]

                lo, hi = 0, TT
                for (so_, s__) in segs:
                    if so_ <= t0 < so_ + s__ + 2 * HALO:
                        lo = max(t0, so_ + HALO) - t0
                        hi = min(t0 + TT, so_ + HALO + s__) - t0
                nr = hi - lo

                def proj(n0, ncols, a=0, b_=TT):
                    i = nxt_pm()
                    S.multi("pe", [MM(pmm[i][:, 0:b_ - a], w3[:, kc, n0:n0 + ncols], xT3[:, kc, a:b_], kc == 0, kc == 7)
                                   for kc in range(8)], R=[RxT[sl]], W=[Rpmm[i]])
                    return i

                c3 = c_sb[sl][:].rearrange("p (c t) -> p c t", c=4)
                q3 = q_sb[sl][:].rearrange("p (c t) -> p c t", c=4)
                k3 = k_sb[sl][:].rearrange("p (c t) -> p c t", c=4)
                v3 = v_sb[sl][:].rearrange("p (c t) -> p c t", c=4)
                g3 = g_sb[sl][:].rearrange("p (c t) -> p c t", c=16)
                if part == 0:
                    for n in ["c", "q", "k", "v", "g"]:
                        Rout[n][sl].newgen()
                    for i in range(4):
                        pg = proj(512 + 128 * i, 128)
                        si = state["sg"] % 2
                        state["sg"] += 1
                        S.op("act", ACT(sg[si][:], pmm[pg][:, 0:TT], AF.Sigmoid), R=[Rpmm[pg]], W=[Rsg[si]])
                        pa = proj(128 * i, 128)
                        S.op("dve", TTOP(c3[:, i, :], pmm[pa][:, 0:TT], sg[si][:], ALU.mult),
                             R=[Rpmm[pa], Rsg[si]], Wacc=[Rout["c"][sl]])
                    for i in range(4):
                        pq = proj(1024 + 128 * i, 128, lo, hi)
                        S.op("act", ACT(q3[:, i, lo:hi], pmm[pq][:, 0:nr], AF.Identity, scale=0.125), R=[Rpmm[pq]], Wacc=[Rout["q"][sl]])
                        pk = proj(1536 + 128 * i, 128)
                        S.op("dve", COPY(k3[:, i, :], pmm[pk][:, 0:TT]), R=[Rpmm[pk]], Wacc=[Rout["k"][sl]])
                    S.dma("pool", DMA(fm(cT_d, t0, TT), c3), f"c{sl}", R=[Rout["c"][sl]])
                    S.dma("pool", DMA(fm(qT_d, t0 + lo, nr), q3[:, :, lo:hi]), f"q{sl}", R=[Rout["q"][sl]])
                    S.dma("pool", DMA(fm(kT_d, t0, TT), k3), f"k{sl}", R=[Rout["k"][sl]])
                else:
                    for b in range(4):
                        i = nxt_pm()
                        S.multi("pe", [MM(pmm[i][:, 0:512], xT3[:, kc, b * 128:(b + 1) * 128], w3[:, kc, 2048:2560], kc == 0, kc == 7)
                                       for kc in range(8)], R=[RxT[sl]], W=[Rpmm[i]])
                        if b % 2 == 0:
                            S.op("dve", COPY(v3[:, b, :], pmm[i][:, 0:512]), R=[Rpmm[i]], Wacc=[Rout["v"][sl]])
                        else:
                            S.op("act", ACT(v3[:, b, :], pmm[i][:, 0:512], AF.Copy), R=[Rpmm[i]], Wacc=[Rout["v"][sl]])
                    for i in range(16):
                        pg = proj(2560 + 128 * i, 128, lo, hi)
                        S.op("act", ACT(g3[:, i, lo:hi], pmm[pg][:, 0:nr], AF.Sigmoid), R=[Rpmm[pg]], Wacc=[Rout["g"][sl]])
                    S.dma("pool", DMA(v_d[t0:t0 + TT, :].rearrange("(b p) f -> p b f", p=128), v3), f"v{sl}", R=[Rout["v"][sl]])
                    S.dma("pool", DMA(fm(gc_d, t0 + lo, nr), g3[:, 0:8, lo:hi]), f"gc{sl}", R=[Rout["g"][sl]])
                    S.dma("pool", DMA(fm(ga_d, t0 + lo, nr), g3[:, 8:16, lo:hi]), f"ga{sl}", R=[Rout["g"][sl]])

            p1_load(0)
            p1_T(0)
            for j in range(ntile):
                if j + 1 < ntile:
                    p1_load(j + 1)
                p1_mm(j, 0)
                if j + 1 < ntile:
                    p1_T(j + 1)
                p1_mm(j, 1)
            S.finish()
            S.replay()

    with contextlib.ExitStack() as st:
        sb = lambda name, shape, dt: st.enter_context(nc.sbuf_tensor(name, list(shape), dt))
        ps = lambda name, shape, dt=F32: st.enter_context(nc.psum_tensor(name, list(shape), dt))
        dwd = sb("dwd", [128, 4 * CK * 128], BF16)
        w_co = sb("w_co", [128, 4 * D], BF16)
        identb = sb("identb2", [128, 128], F32)
        onesf = sb("onesf", [128, 128], F32)
        dwk = sb("dwk", [128, 4 * CK], F32)
        cpar = sb("cpar", [128, 12], F32)
        stage = [sb(f"stg2{i}", [128, 2048], F32) for i in range(2)]
        c_in = [sb(f"c_in{i}", [128, 4 * (TT + 30)], BF16) for i in range(2)]
        gc_in = [sb(f"gc_in{i}", [128, 8 * TT], BF16) for i in range(2)]
        y_sb = sb("y_sb", [128, 4 * TT], F32)
        ysq = sb("ysq", [128, 4 * TT], F32)
        m_sb = sb("m_sb", [128, TT], F32)
        r_sb = sb("r_sb", [128, TT], F32)
        t_sb = [sb(f"t_sb{i}", [128, TT], F32) for i in range(2)]
        z_sb = sb("z_sb", [128, 4 * TT], BF16)
        G1_sb = [sb(f"G1_sb{i}", [128, 8 * TT], BF16) for i in range(2)]
        py = [ps(f"py{i}", [128, 512]) for i in range(4)]
        pst = [ps(f"pst{i}", [128, 512]) for i in range(2)]
        pco = [ps(f"pco{i}", [128, 512]) for i in range(2)]

        with contextlib.ExitStack() as st2:
            S = Sch(nc, st2, "w2", ["wl", "m0", "m1", "m2"])
            Rstage = [Reg(), Reg()]
            Rw = Reg()
            load_cast(S, w_co, w_co_r, None, None, Rw, 4 * D)
            Rid, Rdwk, Rcp = Reg(), Reg(), Reg()
            S.dma("sp", DMA(identb[:], ident_r), "m0", W=[Rid])
            S.dma("sp", DMA(dwk[:], dwk_r), "m1", W=[Rdwk])
            S.dma("sp", DMA(cpar[:], cpar_r), "m2", W=[Rcp])
            Rones = Reg()
            S.op("dve", lambda e: e.memset(onesf[:], 1.0 / 512.0), W=[Rones])
            Rdwd = Reg()
            engs = ["dve", "pool"]
            for ch in range(4):
                for j in range(CK):
                    o = (ch * CK + j) * 128
                    S.op("dve",
                         TSOP(dwd[:, o:o + 128], identb[:], dwk[:, ch * CK + j:ch * CK + j + 1], None, ALU.mult, ALU.bypass),
                         R=[Rid, Rdwk], Wacc=[Rdwd])
            S.finish()
            S.replay()

        with contextlib.ExitStack() as st2:
            dsems = [f"ci{i}" for i in range(2)] + [f"gi{i}" for i in range(2)] + [f"go{i}" for i in range(2)]
            S = Sch(nc, st2, "p2", dsems)
            Rci = [Reg(), Reg()]
            Rgi = [Reg(), Reg()]
            Rgo = [Reg(), Reg()]
            Rpy = [Reg() for _ in range(4)]
            Rpst = [Reg(), Reg()]
            Rpco = [Reg(), Reg()]
            Ry, Rysq, Rm, Rr, Rz = Reg(), Reg(), Reg(), Reg(), Reg()
            Rt = [Reg(), Reg()]
            dwd4 = dwd[:].rearrange("p (c j m) -> p c j m", c=4, j=CK)
            wco3 = w_co[:].rearrange("p (k n) -> p k n", k=4)
            y3 = y_sb[:].rearrange("p (c t) -> p c t", c=4)
            ysq3 = ysq[:].rearrange("p (c t) -> p c t", c=4)
            z3 = z_sb[:].rearrange("p (c t) -> p c t", c=4)
            NT2 = len(real_tiles)

            def v2(ti):
                sl = ti % 2
                return (sl, c_in[sl][:].rearrange("p (c t) -> p c t", c=4),
                        gc_in[sl][:].rearrange("p (c t) -> p c t", c=8),
                        G1_sb[sl][:].rearrange("p (c t) -> p c t", c=8))

            def p2_load(ti):
                e0 = real_tiles[ti][0]
                sl, ci3, gi3, go3 = v2(ti)
                S.dma("sp", DMA(ci3, fm(cT_d, e0 - 15, TT + 30)), f"ci{sl}", W=[Rci[sl]])
                S.dma("sp", DMA(gi3, fm(gc_d, e0, TT)), f"gi{sl}", W=[Rgi[sl]])

            def p2_conv(ti):
                sl, ci3, gi3, go3 = v2(ti)
                for ch in range(4):
                    S.multi("pe", [MM(py[ch][:, 0:TT], dwd4[:, ch, j, :], ci3[:, ch, j:j + TT], j == 0, j == CK - 1)
                                   for j in range(CK)], R=[Rci[sl]], W=[Rpy[ch]])

            def p2_evac(ti):
                Ry.newgen()
                Rysq.newgen()
                for ch in range(4):
                    S.op("act", ACT(y3[:, ch, :], py[ch][:, 0:TT], AF.Identity, bias=cpar[:, ch:ch + 1]), R=[Rpy[ch]], Wacc=[Ry])
                    S.op("act", ACT(ysq3[:, ch, :], py[ch][:, 0:TT], AF.Square, bias=cpar[:, ch:ch + 1]), R=[Rpy[ch]], Wacc=[Rysq])

            def p2_stats(ti):
                S.multi("pe", [MM(pst[0][:, 0:TT], onesf[:], y3[:, ch, :], ch == 0, ch == 3) for ch in range(4)], R=[Ry], W=[Rpst[0]])
                S.multi("pe", [MM(pst[1][:, 0:TT], onesf[:], ysq3[:, ch, :], ch == 0, ch == 3) for ch in range(4)], R=[Rysq], W=[Rpst[1]])

            def p2_chain(ti):
                S.op("dve", COPY(m_sb[:], pst[0][:, 0:TT]), R=[Rpst[0]], W=[Rm])
                S.op("dve", TTOP(r_sb[:], m_sb[:], m_sb[:], ALU.mult), R=[Rm], W=[Rr])
                S.op("dve", STT(r_sb[:], pst[1][:, 0:TT], EPS, r_sb[:], ALU.add, ALU.subtract), R=[Rpst[1]], W=[Rr])
                S.op("act", ACT(r_sb[:], r_sb[:], AF.Sqrt), W=[Rr])
                S.op("dve", lambda e: e.reciprocal(out=r_sb[:], in_=r_sb[:]), W=[Rr])
                Rz.newgen()
                for ch in range(4):
                    ts = t_sb[ch % 2]
                    S.op("dve", TTOP(ts[:], y3[:, ch, :], m_sb[:], ALU.subtract), R=[Ry, Rm], W=[Rt[ch % 2]])
                    S.op("dve", TTOP(ts[:], ts[:], r_sb[:], ALU.mult), R=[Rr], W=[Rt[ch % 2]])
                    S.op("act", ACT(z3[:, ch, :], ts[:], AF.Silu, bias=cpar[:, 8 + ch:9 + ch], scale=cpar[:, 4 + ch:5 + ch]),
                         R=[Rt[ch % 2]], Wacc=[Rz])

            def p2_out(ti):
                e0 = real_tiles[ti][0]
                sl, ci3, gi3, go3 = v2(ti)
                Rgo[sl].newgen()
                for n in range(8):
                    pi = n % 2
                    S.multi("pe", [MM(pco[pi][:, 0:TT], wco3[:, kc, n * 128:(n + 1) * 128], z3[:, kc, :], kc == 0, kc == 3)
                                   for kc in range(4)], R=[Rz], W=[Rpco[pi]])
                    S.op("dve", TTOP(go3[:, n, :], pco[pi][:, 0:TT], gi3[:, n, :], ALU.mult), R=[Rpco[pi], Rgi[sl]], Wacc=[Rgo[sl]])
                S.dma("pool", DMA(fm(G1_d, e0, TT), go3), f"go{sl}", R=[Rgo[sl]])

            p2_load(0)
            if NT2 > 1:
                p2_load(1)
            p2_conv(0)
            p2_evac(0)
            for ti in range(NT2):
                p2_stats(ti)
                if ti + 1 < NT2:
                    p2_conv(ti + 1)
                p2_chain(ti)
                p2_out(ti)
                if ti + 1 < NT2:
                    p2_evac(ti + 1)
                if ti + 2 < NT2:
                    p2_load(ti + 2)
            S.finish()
            S.replay()

    with contextlib.ExitStack() as st:
        sb = lambda name, shape, dt: st.enter_context(nc.sbuf_tensor(name, list(shape), dt))
        ps = lambda name, shape, dt=F32: st.enter_context(nc.psum_tensor(name, list(shape), dt))
        BC = sb("BC", [128, NH * 9 * 128], BF16)
        identb = sb("identb3", [128, 128], BF16)
        onesb = sb("onesb", [128, 64], BF16)
        w_ao = sb("w_ao", [128, 4 * D], BF16)
        w_out = sb("w_out", [128, 8 * D], BF16)
        lnp = sb("lnp3", [128, 2 * D], F32)
        bcm_sb = sb("bcm_sb", [128, 9 * 128], F32)
        stage = [sb(f"stg3{i}", [128, 2048], F32) for i in range(2)]
        q_in2 = [sb(f"q_in{i}", [96, NH * TT], BF16) for i in range(2)]
        k_in2 = [sb(f"k_in{i}", [96, NH * 1024], BF16) for i in range(2)]
        v_in2 = [sb(f"v_in{i}", [128, 8 * 512], BF16) for i in range(2)]
        o_cp = sb("o_cp", [128, TT], F32)
        s_cp = sb("s_cp", [128, TT], F32)
        G1_in = sb("G1_in", [128, 8 * TT], BF16)
        ga_in = sb("ga_in", [128, 8 * TT], BF16)
        pT = [sb(f"pT{i}", [128, 1024], BF16) for i in range(2)]
        rs_sb = sb("rs", [128, TT], F32)
        aT2 = [sb(f"aT{i}", [128, 4 * TT], BF16) for i in range(2)]
        tmpg = [sb(f"tmpg{i}", [128, TT], F32) for i in range(2)]
        mixed = sb("mixed", [128, 8 * TT], BF16)
        s_sb = [sb(f"s_sb{i}", [128, D], F32) for i in range(2)]
        x1_sb = [sb(f"x1_sb{i}", [128, D], F32) for i in range(2)]
        stt = [sb(f"stt{i}", [128, 12], F32) for i in range(2)]
        mv = [sb(f"mv{i}", [128, 4], F32) for i in range(2)]
        psc = [ps(f"psc{i}", [128, 1024]) for i in range(2)]
        pmm = [ps(f"pm3{i}", [128, 512]) for i in range(2)]
        po_o = ps("po_o", [128, 512])
        po_s = ps("po_s", [128, 512])

        with contextlib.ExitStack() as st2:
            S = Sch(nc, st2, "w3", ["wl", "st0", "st1", "m0", "m1"])
            Rstage = [Reg(), Reg()]
            load_cast(S, w_ao, w_ao_r, None, None, Reg(), 4 * D)
            load_cast(S, w_out, w_out_r, None, None, Reg(), 8 * D)
            load_cast(S, identb, ident_r, None, None, Reg(), 128)
            S.dma("sp", DMA(lnp[:], lnp_r[:, 0:2 * D]), "m0", W=[Reg()])
            S.op("pool", lambda e: e.memset(onesb[:], 1.0), W=[Reg()])
            RBC = Reg()
            Rbcm = Reg()
            S.dma("sp", DMA(bcm_sb[:], bcm_r), "m1", W=[Rbcm])
            for h in range(NH):
                s = h % 2
                S.dma("sp", DMA(stage[s][:, 0:1152], rpbx_r[:, h * 1152:(h + 1) * 1152]), f"st{s}", W=[Rstage[s]])
                S.op("dve", TTOP(BC[:, h * 1152:(h + 1) * 1152], stage[s][:, 0:1152], bcm_sb[:], ALU.add),
                     R=[Rstage[s], Rbcm], Wacc=[RBC])
            S.finish()
            S.replay()

        with contextlib.ExitStack() as st2:
            dsems = ["qi0", "ki0", "vi0", "qi1", "ki1", "vi1", "g1i", "gai", "xi0", "xi1", "xo0", "xo1"]
            S = Sch(nc, st2, "p3", dsems)
            Rq2, Rk2, Rv2 = [Reg(), Reg()], [Reg(), Reg()], [Reg(), Reg()]
            Rg1, Rga = Reg(), Reg()
            Rocp, Rscp = Reg(), Reg()
            Rxi = [Reg(), Reg()]
            Rxo = [Reg(), Reg()]
            Rpsc = [Reg(), Reg()]
            Rpo = Reg()
            Rpm = [Reg(), Reg()]
            RpT = [Reg(), Reg()]
            Rrs = Reg()
            RaT2 = [Reg(), Reg()]
            Rmixed = Reg()
            Rtmpg = [Reg(), Reg()]
            Rs = [Reg(), Reg()]
            Rstt = [Reg(), Reg()]
            Rmv = [Reg(), Reg()]
            BC3 = BC[:].rearrange("p (h x) -> p h x", h=NH)
            wao3 = w_ao[:].rearrange("p (c n) -> p c n", c=4)
            wout3 = w_out[:].rearrange("p (k n) -> p k n", k=8)
            g1i3 = G1_in[:].rearrange("p (c t) -> p c t", c=8)
            gai3 = ga_in[:].rearrange("p (c t) -> p c t", c=8)
            mx3 = mixed[:].rearrange("p (c t) -> p c t", c=8)
            cnt = {"pm": 0, "blk": 0, "step": 0}

            def att_specs(first, last):
                steps = [[(3, 0, 3), (4, 0, 3)],
                         [(2, 0, 2), (0, 0, 0), (5, 1, 3), (7, 3, 3)],
                         [(1, 0, 1), (6, 2, 3)]]
                if first:
                    steps[2] = steps[2] + [(5, 0, 0), (6, 0, 0)]
                if last:
                    steps[2] = steps[2] + [(2, 3, 3), (1, 3, 3)]
                out = []
                for L in steps:
                    off = 0
                    sp = []
                    for (kt, lo, hi) in L:
                        w = (hi - lo + 1) * 128
                        assert off // 512 == (off + w - 1) // 512 and off + w <= 1024
                        sp.append((kt, lo, hi, off, w))
                        off += w
                    out.append((sp, off))
                return out

            def views(tj):
                sj = tj % 2
                return (q_in2[sj][:].rearrange("p (h t) -> p h t", h=NH),
                        k_in2[sj][:].rearrange("p (h t) -> p h t", h=NH),
                        v_in2[sj][:].rearrange("p (b f) -> p b f", b=8))

            def issue_qkv(tj):
                e0j = real_tiles[tj][0]
                kwj = e0j - 256
                sj = tj % 2
                q3j, k3j, v3j = views(tj)
                S.dma("sp", DMA(q3j[0:64, :, :], qT_d.rearrange("(h d) t -> d h t", d=64)[:, :, e0j:e0j + TT]), f"qi{sj}", W=[Rq2[sj]])
                for h in range(NH):
                    S.dma("sp", DMA(q3j[64:96, h, :], qm_r[:, e0j:e0j + TT]), f"qi{sj}", Wacc=[Rq2[sj]])
                S.dma("sp", DMA(k3j[0:64, :, :], kT_d.rearrange("(h d) t -> d h t", d=64)[:, :, kwj:kwj + 1024]), f"ki{sj}", W=[Rk2[sj]])
                for h in range(NH):
                    S.dma("sp", DMA(k3j[64:96, h, :], km_r[:, kwj:kwj + 1024]), f"ki{sj}", Wacc=[Rk2[sj]])
                S.dma("sp", DMA(v3j, v_d[kwj:kwj + 1024, :].rearrange("(b p) f -> p b f", p=128)), f"vi{sj}", W=[Rv2[sj]])

            def issue_gates(tj):
                e0j = real_tiles[tj][0]
                S.dma("sp", DMA(g1i3, fm(G1_d, e0j, TT)), "g1i", W=[Rg1])
                S.dma("sp", DMA(gai3, fm(ga_d, e0j, TT)), "gai", W=[Rga])

            all_steps = []
            for tj, (_e, _o, f_, l_) in enumerate(real_tiles):
                sp3 = att_specs(f_, l_)
                for h in range(NH):
                    for si in range(3):
                        all_steps.append((tj, h, si, sp3[si]))

            def emit_scores(gi):
                tj, h, si, (sp, tot) = all_steps[gi]
                q3, k3, v3 = views(tj)
                b = gi % 2
                mms = []
                for (kt, lo, hi, off, w) in sp:
                    o = psc[b][:, off:off + w]
                    mms.append((off // 512, o, k3[0:96, h, kt * 128:(kt + 1) * 128], q3[0:96, h, lo * 128:(hi + 1) * 128]))
                    d0 = 6 - kt + lo
                    mms.append((off // 512, o, identb[:], BC3[:, h, d0 * 128:d0 * 128 + w]))
                fns = []
                for i, (bk, o, l, r_) in enumerate(mms):
                    st_ = all(m[0] != bk for m in mms[:i])
                    sp_ = all(m[0] != bk for m in mms[i + 1:])
                    fns.append(MM(o, l, r_, st_, sp_))
                S.multi("pe", fns, R=[Rq2[tj % 2], Rk2[tj % 2]], W=[Rpsc[b]])

            def emit_pv(gi):
                tj, h, si, (sp, tot) = all_steps[gi]
                q3, k3, v3 = views(tj)
                b = gi % 2
                p0 = (h % 2) * 64
                pr = h // 2
                if h % 2 == 0 and si == 0:
                    Rpo.newgen()
                S.op("act", ACT(pT[b][:, 0:tot], psc[b][:, 0:tot], AF.Exp), R=[Rpsc[b]], W=[RpT[b]])
                fns = []
                nsp = len(sp)
                for dst, lhs_of in ((po_o, lambda kt: v3[:, kt, h * 64:(h + 1) * 64]), (po_s, lambda kt: onesb[:, 0:64])):
                    for i, (kt, lo, hi, off, w) in enumerate(sp):
                        fns.append(MM(dst[p0:p0 + 64, lo * 128:(hi + 1) * 128], lhs_of(kt), pT[b][:, off:off + w],
                                      si == 0 and i == 0, si == 2 and i == nsp - 1))
                S.multi("pe", fns, R=[RpT[b], Rv2[tj % 2]], Wacc=[Rpo])
                if h % 2 == 1 and si == 2:
                    a3 = aT2[tj % 2][:].rearrange("p (c t) -> p c t", c=4)
                    if pr == 0:
                        RaT2[tj % 2].newgen()
                    S.op("act", ACT(s_cp[:], po_s[:, :], AF.Copy), R=[Rpo], W=[Rscp])
                    S.op("act", ACT(o_cp[:], po_o[:, :], AF.Copy), R=[Rpo], W=[Rocp])
                    S.op("dve", lambda e: e.reciprocal(out=rs_sb[:], in_=s_cp[:]), R=[Rscp], W=[Rrs])
                    S.op("dve", TTOP(a3[:, pr, :], o_cp[:], rs_sb[:], ALU.mult), R=[Rocp, Rrs], Wacc=[RaT2[tj % 2]])

            def tail_ops(tj):
                e0 = real_tiles[tj][0]
                a3 = aT2[tj % 2][:].rearrange("p (c t) -> p c t", c=4)
                RaT = RaT2[tj % 2]
                ops = []

                def att_out(nn):
                    if nn == 0:
                        Rmixed.newgen()
                    pi = cnt["pm"] % 2
                    cnt["pm"] += 1
                    S.multi("pe", [MM(pmm[pi][:, 0:TT], wao3[:, c_, nn * 128:(nn + 1) * 128], a3[:, c_, :], c_ == 0, c_ == 3)
                                   for c_ in range(4)], R=[RaT], W=[Rpm[pi]])
                    S.op("dve", TTOP(tmpg[pi][:], pmm[pi][:, 0:TT], gai3[:, nn, :], ALU.mult), R=[Rpm[pi], Rga], W=[Rtmpg[pi]])
                    S.op("pool", TTOP(mx3[:, nn, :], tmpg[pi][:], g1i3[:, nn, :], ALU.add), R=[Rtmpg[pi], Rg1], Wacc=[Rmixed])
                    if nn == 7 and tj + 1 < len(real_tiles):
                        issue_gates(tj + 1)

                def w_out_half(b, half):
                    if half == 0:
                        cnt["blk"] += 1
                    bi = (cnt["blk"] - 1) % 2
                    if half == 0:
                        S.dma("sp", DMA(stage[bi][:, 0:D], x_ext[e0 + b * 128:e0 + (b + 1) * 128, :]), f"xi{bi}", W=[Rxi[bi]])
                    pi = cnt["pm"] % 2
                    cnt["pm"] += 1
                    S.multi("pe", [MM(pmm[pi][:, 0:512], mx3[:, kc, b * 128:(b + 1) * 128], wout3[:, kc, half * 512:(half + 1) * 512], kc == 0, kc == 7)
                                   for kc in range(8)], R=[Rmixed], W=[Rpm[pi]])
                    S.op("dve", STT(s_sb[bi][:, half * 512:(half + 1) * 512], stage[bi][:, half * 512:(half + 1) * 512], ALPHA,
                                    pmm[pi][:, 0:512], ALU.mult, ALU.add), R=[Rxi[bi], Rpm[pi]],
                         **({"W": [Rs[bi]]} if half == 0 else {"Wacc": [Rs[bi]]}))
                    S.op("dve", lambda e, bi=bi, half=half: e.bn_stats(out=stt[bi][:, half * 6:(half + 1) * 6],
                                                                      in_=s_sb[bi][:, half * 512:(half + 1) * 512]),
                         R=[Rs[bi]], **({"W": [Rstt[bi]]} if half == 0 else {"Wacc": [Rstt[bi]]}))
                    if half == 1:
                        def store(bi=bi, b=b):
                            S.dma("pool", DMA(x1_d[e0 + b * 128:e0 + (b + 1) * 128, :], x1_sb[bi][:]), f"xo{bi}", R=[Rxo[bi]])
                        stg = layer_norm_stages(S, s_sb[bi], stt[bi], mv[bi], x1_sb[bi], lnp[:, 0:D], lnp[:, D:2 * D],
                                                Rs[bi], Rstt[bi], Rmv[bi], Rxo[bi], after=store, lnexp=True)
                        stg[0]()
                        later.append([1, stg[1]])
                        later.append([2, stg[2]])

                for nn in range(8):
                    ops.append(lambda nn=nn: att_out(nn))
                for b in range(4):
                    for half in range(2):
                        ops.append(lambda b=b, half=half: w_out_half(b, half))
                return ops

            later = []

            def run_later(flush=False):
                keep = []
                for item in later:
                    item[0] -= 1
                    if item[0] <= 0 or flush:
                        item[1]()
                    else:
                        keep.append(item)
                later[:] = keep

            NT = len(real_tiles)
            issue_qkv(0)
            issue_gates(0)
            emit_scores(0)
            pending = []
            for gi in range(len(all_steps)):
                tj, h, si, _sp = all_steps[gi]
                if h == 0 and si == 0 and tj + 1 < NT:
                    issue_qkv(tj + 1)
                if gi + 1 < len(all_steps):
                    emit_scores(gi + 1)
                emit_pv(gi)
                run_later()
                if pending and (gi % 3 != 2):
                    pending.pop(0)()
                if h == NH - 1 and si == 2:
                    while pending:
                        pending.pop(0)()
                        run_later()
                    pending = tail_ops(tj)
            while pending:
                pending.pop(0)()
                run_later()
            run_later(flush=True)
            run_later(flush=True)
            S.finish()
            S.replay()

    with contextlib.ExitStack() as st:
        sb = lambda name, shape, dt: st.enter_context(nc.sbuf_tensor(name, list(shape), dt))
        ps = lambda name, shape, dt=F32: st.enter_context(nc.psum_tensor(name, list(shape), dt))
        T4 = 256
        w_up = sb("w_up", [128, 8 * DFF], BF16)
        w_dn = sb("w_dn", [128, 32 * D], BF16)
        ident = sb("ident4", [128, 128], BF16)
        x1b = [sb(f"x1b{i}", [128, D], BF16) for i in range(2)]
        lnp = sb("lnp4", [128, 2 * D], F32)
        x1_in = [sb(f"x1_in{i}", [128, D], F32) for i in range(4)]
        x1T = sb("x1T", [128, 8 * T4], BF16)
        hT = sb("hT", [128, 32 * T4], BF16)
        rl = [sb(f"rl{i}", [128, T4], F32) for i in range(2)]
        s_sb = [sb(f"s4_{i}", [128, D], F32) for i in range(2)]
        o_sb = [sb(f"o4_{i}", [128, D], F32) for i in range(2)]
        stt = [sb(f"stt4{i}", [128, 12], F32) for i in range(2)]
        mv = [sb(f"mv4{i}", [128, 4], F32) for i in range(2)]
        ptr = [ps(f"ptr4{i}", [128, 1024], BF16) for i in range(2)]
        pup = [ps(f"pup{i}", [128, 512]) for i in range(2)]
        pdn = [ps(f"pdn{i}", [128, 512]) for i in range(4)]

        with contextlib.ExitStack() as st2:
            S = Sch(nc, st2, "w4", ["wl", "st0", "st1", "st2", "st3", "m1"])
            stage = [s_sb[0], s_sb[1], o_sb[0], o_sb[1]]
            Rstage = [Reg() for _ in range(4)]
            load_cast(S, w_up, w_up_r, stage, Rstage, Reg(), 8 * DFF)
            load_cast(S, w_dn, w_dn_r, stage, Rstage, Reg(), 32 * D)
            load_cast(S, ident, ident_r, None, None, Reg(), 128)
            S.dma("sp", DMA(lnp[:], lnp_r[:, 2 * D:4 * D]), "m1", W=[Reg()])
            S.finish()
            S.replay()

        with contextlib.ExitStack() as st2:
            dsems = [f"xi{i}" for i in range(4)] + ["oo0", "oo1"]
            S = Sch(nc, st2, "p4", dsems)
            Rxi = [Reg() for _ in range(4)]
            Rptr = [Reg(), Reg()]
            Rpup = [Reg(), Reg()]
            Rpdn = [Reg() for _ in range(4)]
            RxT, RhT = Reg(), Reg()
            Rrl = [Reg(), Reg()]
            Rs = [Reg(), Reg()]
            Ro = [Reg(), Reg()]
            Rstt = [Reg(), Reg()]
            Rmv = [Reg(), Reg()]
            xT3 = x1T[:].rearrange("p (k t) -> p k t", k=8)
            hT3 = hT[:].rearrange("p (f t) -> p f t", f=32)
            wup3 = w_up[:].rearrange("p (k n) -> p k n", k=8)
            wdn3 = w_dn[:].rearrange("p (f n) -> p f n", f=32)
            cnt = {"pt": 0, "pu": 0, "pd": 0, "blk": 0, "g": 0}
            subs = []
            for (e0, o0_, _f, _l) in real_tiles:
                for sub in range(TT // T4):
                    subs.append((e0 + sub * T4, o0_ + sub * T4))
            slots_of = {}

            def emit_load(k):
                es, _ = subs[k]
                sl_ = []
                for b in range(2):
                    s_ = cnt["g"] % 4
                    cnt["g"] += 1
                    sl_.append(s_)
                    S.dma("sp", DMA(x1_in[s_][:], x1_d[es + b * 128:es + (b + 1) * 128, :]), f"xi{s_}", W=[Rxi[s_]])
                slots_of[k] = sl_

            Rx1b = [Reg(), Reg()]

            def emit_transposes(k):
                RxT.newgen()
                for b in range(2):
                    s_ = slots_of[k][b]
                    S.op("dve", COPY(x1b[b][:], x1_in[s_][:]), R=[Rxi[s_]], W=[Rx1b[b]])
                    pi = cnt["pt"] % 2
                    cnt["pt"] += 1
                    S.multi("pe", [TR(ptr[pi][:, i * 128:(i + 1) * 128], x1b[b][:, i * 128:(i + 1) * 128], ident[:])
                                   for i in range(8)], R=[Rx1b[b]], W=[Rptr[pi]])
                    src = ptr[pi][:].rearrange("p (k t) -> p k t", k=8)
                    dst = xT3[:, :, b * 128:(b + 1) * 128]
                    if b == 0:
                        S.op("act", ACT(dst, src, AF.Copy), R=[Rptr[pi]], Wacc=[RxT])
                    else:
                        S.op("dve", COPY(dst, src), R=[Rptr[pi]], Wacc=[RxT])

            later4 = {}

            def emit_up(k):
                RhT.newgen()
                for f in range(32):
                    pi = cnt["pu"] % 2
                    cnt["pu"] += 1
                    S.multi("pe", [MM(pup[pi][:, 0:T4], wup3[:, kc, f * 128:(f + 1) * 128], xT3[:, kc, :], kc == 0, kc == 7)
                                   for kc in range(8)], R=[RxT], W=[Rpup[pi]])
                    S.op("act", ACT(rl[pi][:], pup[pi][:, 0:T4], AF.Relu), R=[Rpup[pi]], W=[Rrl[pi]])
                    S.op("dve", TTOP(hT3[:, f, :], rl[pi][:], rl[pi][:], ALU.mult), R=[Rrl[pi]], Wacc=[RhT])
                    for fn in later4.pop(f, []):
                        fn()

            def emit_down(k, last_sub):
                _, os_ = subs[k]
                for b in range(2):
                    s_ = slots_of[k][b]
                    bi = cnt["blk"] % 2
                    cnt["blk"] += 1
                    for half in range(2):
                        pi = cnt["pd"] % 4
                        cnt["pd"] += 1
                        S.multi("pe", [MM(pdn[pi][:, 0:512], hT3[:, f, b * 128:(b + 1) * 128], wdn3[:, f, half * 512:(half + 1) * 512], f == 0, f == 31)
                                       for f in range(32)], R=[RhT], W=[Rpdn[pi]])
                        S.op("dve", STT(s_sb[bi][:, half * 512:(half + 1) * 512], x1_in[s_][:, half * 512:(half + 1) * 512], ALPHA,
                                        pdn[pi][:, 0:512], ALU.mult, ALU.add), R=[Rxi[s_], Rpdn[pi]],
                             **({"W": [Rs[bi]]} if half == 0 else {"Wacc": [Rs[bi]]}))
                        S.op("dve", lambda e, bi=bi, half=half: e.bn_stats(out=stt[bi][:, half * 6:(half + 1) * 6],
                                                                          in_=s_sb[bi][:, half * 512:(half + 1) * 512]),
                             R=[Rs[bi]], **({"W": [Rstt[bi]]} if half == 0 else {"Wacc": [Rstt[bi]]}))

                    def store(bi=bi, b=b):
                        S.dma("pool", DMA(out_d[os_ + b * 128:os_ + (b + 1) * 128, :], o_sb[bi][:]), f"oo{bi}", R=[Ro[bi]])
                    stg = layer_norm_stages(S, s_sb[bi], stt[bi], mv[bi], o_sb[bi], lnp[:, 0:D], lnp[:, D:2 * D],
                                            Rs[bi], Rstt[bi], Rmv[bi], Ro[bi], after=store)
                    stg[0]()
                    if b == 0 or last_sub:
                        stg[1]()
                        stg[2]()
                    else:
                        later4.setdefault(3, []).append(stg[1])
                        later4.setdefault(6, []).append(stg[2])

            nsub = len(subs)
            emit_load(0)
            emit_transposes(0)
            for k in range(nsub):
                if k + 1 < nsub:
                    emit_load(k + 1)
                emit_up(k)
                if k + 1 < nsub:
                    emit_transposes(k + 1)
                emit_down(k, k == nsub - 1)
            S.finish()
            S.replay()
    return nc


def layer_norm_stages(S, s, stt, mv, out, g, b, Rs, Rstt, Rmv, Rout, after=None, lnexp=False):
    def st1():
        S.op("dve", lambda e: e.bn_aggr(out=mv[:, 0:2], in_=stt[:, 0:12]), R=[Rstt], W=[Rmv])
        S.op("dve", TSOP(mv[:, 2:3], mv[:, 1:2], EPS, None, ALU.add, ALU.bypass), W=[Rmv])

    def st2():
        if lnexp:
            S.op("act", ACT(mv[:, 2:3], mv[:, 2:3], AF.Ln), W=[Rmv])
            S.op("act", ACT(mv[:, 2:3], mv[:, 2:3], AF.Exp, scale=-0.5), W=[Rmv])
        else:
            S.op("act", ACT(mv[:, 2:3], mv[:, 2:3], AF.Sqrt), W=[Rmv])
            S.op("dve", lambda e: e.reciprocal(out=mv[:, 2:3], in_=mv[:, 2:3]), W=[Rmv])
        S.op("dve", STT(mv[:, 3:4], mv[:, 0:1], -1.0, mv[:, 2:3], ALU.mult, ALU.mult), W=[Rmv])

    def st3():
        S.op("act", ACT(s[:], s[:], AF.Identity, bias=mv[:, 3:4], scale=mv[:, 2:3]), R=[Rmv], W=[Rs])
        S.op("pool", TTOP(s[:], s[:], g, ALU.mult), W=[Rs])
        S.op("pool", TTOP(out[:], s[:], b, ALU.add), R=[Rs], W=[Rout])
        if after is not None:
            after()
    return [st1, st2, st3]


def layer_norm_tail(S, s, stt, mv, out, g, b, Rs, Rstt, Rmv, Rout):
    for f in layer_norm_stages(S, s, stt, mv, out, g, b, Rs, Rstt, Rmv, Rout):
        f()


def _bc_tables(rpb):
    a = np.arange(2)[:, None, None, None, None]
    kc = np.arange(64)[None, :, None, None, None]
    dl = np.arange(4, -5, -1)[None, None, :, None, None]
    b = np.arange(2)[None, None, None, :, None]
    qc = np.arange(64)[None, None, None, None, :]
    dr = 2 * dl + a - b + 0 * kc + 0 * qc
    cs = np.clip(qc - 8, 0, 48)
    valid = (np.abs(dr) <= 7) & (kc >= cs) & (kc < cs + 16)
    dri = np.clip(dr + 7, 0, 14)
    dci = np.clip(kc - qc + 15, 0, 30) + 0 * dr
    dri, dci, valid = np.broadcast_arrays(dri, dci, valid)
    g = rpb[:, dri, dci]
    g = g.transpose(1, 2, 0, 3, 4, 5).reshape(128, NH * 9 * 128)
    m = np.where(valid, np.float32(0.0), np.float32(NEG)).astype(np.float32).reshape(128, 9 * 128)
    return np.ascontiguousarray(g.astype(np.float32)), np.ascontiguousarray(m)


def _row_tables(segs_info, TEXT):
    km = np.zeros((32, TEXT), np.float32)
    qm = np.zeros((32, TEXT), np.float32)
    for (so, s_, roff, rseq) in segs_info:
        e = np.arange(so, so + s_ + 2 * HALO)
        rl = np.floor_divide(e - so - HALO, GRID_W)
        rg = rl + roff
        km[np.mod(rg, 32), e] = 1.0
        rs = np.clip(rg - 4, 0, rseq - 8)
        res = np.arange(32)[:, None]
        kr = rg[None, :] - 15 + np.mod(res - (rg[None, :] - 15), 32)
        ok = (kr >= rs[None, :]) & (kr < rs[None, :] + 8)
        qm[:, e] = np.where(ok, 0.0, NEG)
    return km.astype(ml_dtypes.bfloat16), qm.astype(ml_dtypes.bfloat16)


def _host_common(w_in, dw_kernel, dw_bias, conv_ln_g, conv_ln_b, w_conv_out, rpb, w_att_out,
                 w_out, ln1_g, ln1_b, w_up, w_down, ln2_g, ln2_b):
    f = np.float32
    r = lambda w, k: np.ascontiguousarray(np.asarray(w, f).reshape(k, 128, -1).transpose(1, 0, 2).reshape(128, -1))
    com = {}
    com["w_in_r"] = r(w_in[0], 8)
    com["w_co_r"] = r(w_conv_out[0], 4)
    com["w_ao_r"] = r(w_att_out[0], 4)
    com["w_out_r"] = r(w_out[0], 8)
    com["w_up_r"] = r(w_up[0], 8)
    com["w_dn_r"] = r(w_down[0], 32)
    com["dwk_r"] = np.ascontiguousarray(np.asarray(dw_kernel[0], f).reshape(CK, 4, 128).transpose(2, 1, 0).reshape(128, 4 * CK))
    cp = np.stack([np.asarray(v[0], f).reshape(4, 128).T for v in (dw_bias, conv_ln_g, conv_ln_b)], axis=1)
    com["cpar_r"] = np.ascontiguousarray(cp.reshape(128, 12))
    lnp = np.concatenate([np.asarray(v[0], f) for v in (ln1_g, ln1_b, ln2_g, ln2_b)])[None, :]
    com["lnp_r"] = np.ascontiguousarray(np.broadcast_to(lnp, (128, 4 * D)))
    com["ident_r"] = np.eye(128, dtype=f)
    g, m = _bc_tables(np.asarray(rpb[0], f))
    com["rpbx_r"] = g
    com["bcm_r"] = m
    return com


def _prepare(x_prompt, x_sample, w_in, dw_kernel, dw_bias, conv_ln_g, conv_ln_b, w_conv_out, rpb,
             w_att_out, w_out, ln1_g, ln1_b, w_up, w_down, ln2_g, ln2_b):
    x_prompt = np.asarray(x_prompt, np.float32)
    x_sample = np.asarray(x_sample, np.float32)
    NB, SEQ, _ = x_prompt.shape
    DB, DSEQ, _ = x_sample.shape
    ncores = DB
    parts = ncores // NB
    SB = SEQ // parts
    SA = DSEQ
    com = _host_common(w_in, dw_kernel, dw_bias, conv_ln_g, conv_ln_b, w_conv_out, rpb, w_att_out,
                       w_out, ln1_g, ln1_b, w_up, w_down, ln2_g, ln2_b)
    EA, EB = SA + 2 * HALO, SB + 2 * HALO
    TEXT = EA + EB
    in_maps = []
    for c in range(ncores):
        pi, qi = c // parts, c % parts
        xe = np.zeros((TEXT, D), np.float32)
        xe[HALO:HALO + SA] = x_sample[c]
        lo, hi = qi * SB - HALO, (qi + 1) * SB + HALO
        slo, shi = max(lo, 0), min(hi, SEQ)
        xe[EA + (slo - lo):EA + (slo - lo) + (shi - slo)] = x_prompt[pi, slo:shi]
        km, qm = _row_tables([(0, SA, 0, SA // GRID_W), (EA, SB, qi * (SB // GRID_W), SEQ // GRID_W)], TEXT)
        m = dict(com)
        m["x_ext"] = xe
        m["km_r"] = km
        m["qm_r"] = qm
        in_maps.append(m)

    def assemble(results):
        y_prompt = np.zeros_like(x_prompt)
        y_sample = np.zeros_like(x_sample)
        for c in range(ncores):
            if results[c] is None:
                continue
            o = np.asarray(results[c]["out"], np.float32)
            pi, qi = c // parts, c % parts
            y_sample[c] = o[0:SA]
            y_prompt[pi, qi * SB:(qi + 1) * SB] = o[SA:SA + SB]
        return (y_prompt, y_sample)
    return SA, SB, in_maps, assemble


def kernel(**inputs):
    SA, SB, in_maps, assemble = _prepare(**inputs)
    nc = build_program(SA, SB)
    res = run_bass_kernel_spmd(nc, in_maps, core_ids=list(range(len(in_maps))))
    return assemble(res.results)
```
